# Optimizing a Trainium2 kernel written in Bass

```python
import math
import jax, jax.numpy as jnp
from jax import lax
import numpy as np

D_MODEL = 1024
BATCH = 8
SEQ = 4096
DEPTH = 2

CTX_LEN = 256
GRID_W = 64
N_HEADS = 8
QK_NOPE = 64
QK_ROPE = 32
V_DIM = 64
Q_LORA = 256
KV_LORA = 256
ROPE_THETA = 10000.0
Q_BLOCK = 128
ATTN_SCALE = 1.0 / math.sqrt(QK_NOPE + QK_ROPE)
CONV_WIDTH = 512
CONV_K = 3
S5_WIDTH = 512
S5_GROUP = 16
S5_GROUPS = S5_WIDTH // S5_GROUP
S5_STATE = 64
DT_MIN = 1e-3
DT_MAX = 1e-1
N_BRANCH = 3
D_FF = 4 * D_MODEL
EPS = 1e-6
OFF_Q = 0
OFF_KV = OFF_Q + Q_LORA
OFF_PE = OFF_KV + KV_LORA
OFF_CB = OFF_PE + QK_ROPE
OFF_CC = OFF_CB + CONV_WIDTH
OFF_CX = OFF_CC + CONV_WIDTH
OFF_S5 = OFF_CX + CONV_WIDTH
OFF_G = OFF_S5 + S5_WIDTH
IN_COLS = OFF_G + N_BRANCH * D_MODEL

kernel_name = 'hybrid_mla_shortconv_s5_dit_block'


def rms_norm(x, g):
    xf = x.astype(jnp.float32)
    y = xf * lax.rsqrt(jnp.mean(xf * xf, axis=-1, keepdims=True) + EPS)
    return (y * g.astype(jnp.float32)).astype(x.dtype)


def modulate(x, shift, scale):
    return x * (1 + scale) + shift


def squared_relu_mlp(x, w1, w2):
    return jnp.square(jax.nn.relu(x @ w1)) @ w2


def axial_rope_tables(n_tokens, dtype):
    rows = n_tokens // GRID_W
    row = jnp.repeat(jnp.arange(rows), GRID_W)
    col = jnp.tile(jnp.arange(GRID_W), rows)
    pos = jnp.stack([row, col], axis=-1).astype(jnp.float32)
    n_freq = QK_ROPE // 4
    inv = ROPE_THETA ** (-jnp.arange(n_freq, dtype=jnp.float32) / n_freq)
    ang = pos[:, :, None, None] * inv[None, None, None, :]
    ang = jnp.broadcast_to(ang, (n_tokens, 2, 2, n_freq)).reshape(n_tokens, QK_ROPE)
    return jnp.cos(ang).astype(dtype), jnp.sin(ang).astype(dtype)


def apply_rope(x, cos, sin):
    xr = x.reshape(x.shape[:-1] + (2, 2, QK_ROPE // 4))
    rot = jnp.concatenate([-xr[..., 1:, :], xr[..., :1, :]], axis=-2).reshape(x.shape)
    return x * cos + rot * sin


def mla_queries(z_q, q_norm, w_uq, rope):
    b, n, _ = z_q.shape
    q = (rms_norm(z_q, q_norm) @ w_uq).reshape(b, n, N_HEADS, QK_NOPE + QK_ROPE).transpose(0, 2, 1, 3)
    q_nope, q_pe = q[..., :QK_NOPE], q[..., QK_NOPE:]
    if rope is not None:
        q_pe = apply_rope(q_pe, rope[0], rope[1])
    return jnp.concatenate([q_nope, q_pe], axis=-1)


def mla_keys_values(z_kv, z_pe, kv_norm, w_ukv, rope):
    b, n, _ = z_kv.shape
    kv = (rms_norm(z_kv, kv_norm) @ w_ukv).reshape(b, n, N_HEADS, QK_NOPE + V_DIM).transpose(0, 2, 1, 3)
    k_nope, v = kv[..., :QK_NOPE], kv[..., QK_NOPE:]
    k_pe = z_pe[:, None]
    if rope is not None:
        k_pe = apply_rope(k_pe, rope[0], rope[1])
    k = jnp.concatenate([k_nope, jnp.broadcast_to(k_pe, (b, N_HEADS, n, QK_ROPE))], axis=-1)
    return k, v


def softmax_attend(q, k, v):
    s = jnp.einsum('bhqd,bhkd->bhqk', q, k).astype(jnp.float32) * ATTN_SCALE
    p = jax.nn.softmax(s, axis=-1).astype(v.dtype)
    return jnp.einsum('bhqk,bhkd->bhqd', p, v)


def blocked_attend(q, k, v):
    b, h, n, dk = q.shape
    nb = n // Q_BLOCK
    qb = q.reshape(b, h, nb, Q_BLOCK, dk).transpose(2, 0, 1, 3, 4)
    ob = lax.map(lambda qi: softmax_attend(qi, k, v), qb)
    return ob.transpose(1, 2, 0, 3, 4).reshape(b, h, n, V_DIM)


def merge_heads(o):
    b, h, n, d = o.shape
    return o.transpose(0, 2, 1, 3).reshape(b, n, h * d)


def short_conv(z_b, z_c, z_x, conv_w, conv_w_out):
    u = z_c * z_x
    n = u.shape[1]
    pad = CONV_K // 2
    up = jnp.pad(u, ((0, 0), (pad, pad), (0, 0)))
    y = up[:, 0:n] * conv_w[0]
    for j in range(1, CONV_K):
        y = y + up[:, j:j + n] * conv_w[j]
    return (z_b * y) @ conv_w_out


def cmul(ar, ai, br, bi):
    return ar * br - ai * bi, ar * bi + ai * br


def s5_discretise(a_re, a_im, log_dt, b_re, b_im):
    dt = jnp.exp(log_dt.astype(jnp.float32))[:, None]
    a_re = a_re.astype(jnp.float32)
    a_im = a_im.astype(jnp.float32)
    mag = jnp.exp(dt * a_re)
    ab_re, ab_im = mag * jnp.cos(dt * a_im), mag * jnp.sin(dt * a_im)
    den = a_re * a_re + a_im * a_im
    nr, ni = ab_re - 1.0, ab_im
    f_re = (nr * a_re + ni * a_im) / den
    f_im = (ni * a_re - nr * a_im) / den
    bb_re, bb_im = cmul(f_re[..., None], f_im[..., None], b_re.astype(jnp.float32), b_im.astype(jnp.float32))
    return ab_re, ab_im, bb_re, bb_im


def diag_scan(ab_re, ab_im, b_re, b_im):
    n = b_re.shape[1]
    a_re = jnp.broadcast_to(ab_re, (1, n) + ab_re.shape)
    a_im = jnp.broadcast_to(ab_im, (1, n) + ab_im.shape)

    def combine(e1, e2):
        a1r, a1i, b1r, b1i = e1
        a2r, a2i, b2r, b2i = e2
        ar, ai = cmul(a2r, a2i, a1r, a1i)
        br, bi = cmul(a2r, a2i, b1r, b1i)
        return ar, ai, br + b2r, bi + b2i

    _, _, s_re, s_im = lax.associative_scan(combine, (a_re, a_im, b_re, b_im), axis=1)
    return s_re, s_im


def s5_direction(u_c, u_l, a_re, a_im, log_dt, b_re, b_im, c_re, c_im, reverse, need_ctx_out):
    ab_re, ab_im, bb_re, bb_im = s5_discretise(a_re, a_im, log_dt, b_re, b_im)
    c_re = c_re.astype(jnp.float32)
    c_im = c_im.astype(jnp.float32)

    def flip(t):
        return t[:, ::-1] if reverse else t

    def drive(u):
        return jnp.einsum('blgc,gpc->blgp', u, bb_re), jnp.einsum('blgc,gpc->blgp', u, bb_im)

    def readout(s_re, s_im):
        return jnp.einsum('gcp,blgp->blgc', c_re, s_re) - jnp.einsum('gcp,blgp->blgc', c_im, s_im)

    bc_re, bc_im = drive(flip(u_c))
    sc_re, sc_im = diag_scan(ab_re, ab_im, bc_re, bc_im)
    i_re, i_im = cmul(ab_re, ab_im, sc_re[:, -1], sc_im[:, -1])
    bl_re, bl_im = drive(flip(u_l))
    bl_re = bl_re.at[:, 0].add(i_re)
    bl_im = bl_im.at[:, 0].add(i_im)
    sl_re, sl_im = diag_scan(ab_re, ab_im, bl_re, bl_im)
    y_l = flip(readout(sl_re, sl_im))
    y_c = flip(readout(sc_re, sc_im)) if need_ctx_out else None
    return y_l, y_c


def s5_branch(u_c, u_l, a_re, a_im, log_dt, b_re, b_im, c_re, c_im, s5_d, w_glu, need_ctx_out):
    dtype = u_l.dtype

    def groups(u):
        return u.astype(jnp.float32).reshape(u.shape[0], u.shape[1], S5_GROUPS, S5_GROUP)

    gc, gl = groups(u_c), groups(u_l)
    d = s5_d.astype(jnp.float32).reshape(S5_GROUPS, S5_GROUP)
    y_l = d * gl
    y_c = d * gc if need_ctx_out else None
    for direction in range(2):
        yl_d, yc_d = s5_direction(gc, gl, a_re[direction], a_im[direction], log_dt[direction],
                                  b_re[direction], b_im[direction], c_re[direction], c_im[direction],
                                  direction == 1, need_ctx_out)
        y_l = y_l + yl_d
        if need_ctx_out:
            y_c = y_c + yc_d

    def glu(y):
        z = jax.nn.gelu(y.reshape(y.shape[0], y.shape[1], S5_WIDTH)).astype(dtype) @ w_glu
        return z[..., :D_MODEL] * jax.nn.sigmoid(z[..., D_MODEL:])

    return glu(y_l), (glu(y_c) if need_ctx_out else None)


def gated_merge(z_gate, y_att, y_conv, y_s5, w_out):
    b, n, _ = z_gate.shape
    g = jax.nn.sigmoid(z_gate).reshape(b, n, N_BRANCH, D_MODEL)
    return (g[:, :, 0] * y_att + g[:, :, 1] * y_conv + g[:, :, 2] * y_s5) @ w_out


def token_mixers(xn, hn, w_in, q_norm, w_uq, kv_norm, w_ukv, w_o, conv_w, conv_w_out,
                 a_re, a_im, log_dt, b_re, b_im, c_re, c_im, s5_d, w_glu, w_out, rope, need_ctx_out):
    zx = xn @ w_in
    if need_ctx_out:
        zh = hn @ w_in
        zh_kvpe, zh_s5 = zh[..., OFF_KV:OFF_CB], zh[..., OFF_S5:OFF_G]
    else:
        zh_kvpe = hn @ w_in[:, OFF_KV:OFF_CB]
        zh_s5 = hn @ w_in[:, OFF_S5:OFF_G]
    k_c, v_c = mla_keys_values(zh_kvpe[..., :KV_LORA], zh_kvpe[..., KV_LORA:], kv_norm, w_ukv, None)
    k_x, v_x = mla_keys_values(zx[..., OFF_KV:OFF_PE], zx[..., OFF_PE:OFF_CB], kv_norm, w_ukv, rope)
    q_x = mla_queries(zx[..., OFF_Q:OFF_KV], q_norm, w_uq, rope)
    k_all = jnp.concatenate([k_c, k_x], axis=2)
    v_all = jnp.concatenate([v_c, v_x], axis=2)
    att_x = merge_heads(blocked_attend(q_x, k_all, v_all)) @ w_o
    conv_x = short_conv(zx[..., OFF_CB:OFF_CC], zx[..., OFF_CC:OFF_CX], zx[..., OFF_CX:OFF_S5], conv_w, conv_w_out)
    s5_x, s5_h = s5_branch(zh_s5, zx[..., OFF_S5:OFF_G], a_re, a_im, log_dt, b_re, b_im, c_re, c_im,
                           s5_d, w_glu, need_ctx_out)
    y_x = gated_merge(zx[..., OFF_G:], att_x, conv_x, s5_x, w_out)
    if not need_ctx_out:
        return y_x, None
    q_h = mla_queries(zh[..., OFF_Q:OFF_KV], q_norm, w_uq, None)
    att_h = merge_heads(softmax_attend(q_h, k_c, v_c)) @ w_o
    conv_h = short_conv(zh[..., OFF_CB:OFF_CC], zh[..., OFF_CC:OFF_CX], zh[..., OFF_CX:OFF_S5], conv_w, conv_w_out)
    y_h = gated_merge(zh[..., OFF_G:], att_h, conv_h, s5_h, w_out)
    return y_x, y_h


def setup_inputs(seed: int = 0) -> dict:
    key = jax.random.key(seed)
    ks = jax.random.split(key, 32)
    f32 = jnp.float32

    def nrm(k, shape, scale):
        return jax.random.normal(k, shape, f32) * scale

    def gain(k, shape):
        return 1.0 + 0.02 * jax.random.normal(k, shape, f32)

    n_idx = jnp.arange(S5_STATE, dtype=f32)
    s5_a_re = -0.5 + 0.01 * jax.random.normal(ks[14], (DEPTH, 2, S5_GROUPS, S5_STATE), f32)
    s5_a_im = math.pi * n_idx + 0.01 * jax.random.normal(ks[15], (DEPTH, 2, S5_GROUPS, S5_STATE), f32)
    s5_log_dt = jax.random.uniform(ks[16], (DEPTH, 2, S5_GROUPS), f32, math.log(DT_MIN), math.log(DT_MAX))
    return {
        'x': nrm(ks[0], (BATCH, SEQ, D_MODEL), 1.0),
        'c': nrm(ks[1], (BATCH, D_MODEL), 1.0),
        'ctx': nrm(ks[2], (BATCH, CTX_LEN, D_MODEL), 1.0),
        'c_ctx': nrm(ks[3], (D_MODEL,), 1.0),
        'ada_w': nrm(ks[4], (DEPTH, D_MODEL, 6 * D_MODEL), 0.02),
        'ada_b': nrm(ks[5], (DEPTH, 6 * D_MODEL), 0.01),
        'norm_mix': gain(ks[6], (DEPTH, D_MODEL)),
        'w_in': nrm(ks[7], (DEPTH, D_MODEL, IN_COLS), D_MODEL ** -0.5),
        'mla_q_norm': gain(ks[8], (DEPTH, Q_LORA)),
        'mla_w_uq': nrm(ks[9], (DEPTH, Q_LORA, N_HEADS * (QK_NOPE + QK_ROPE)), Q_LORA ** -0.5),
        'mla_kv_norm': gain(ks[10], (DEPTH, KV_LORA)),
        'mla_w_ukv': nrm(ks[11], (DEPTH, KV_LORA, N_HEADS * (QK_NOPE + V_DIM)), KV_LORA ** -0.5),
        'mla_w_o': nrm(ks[12], (DEPTH, N_HEADS * V_DIM, D_MODEL), (N_HEADS * V_DIM) ** -0.5),
        'conv_w': nrm(ks[13], (DEPTH, CONV_K, CONV_WIDTH), CONV_K ** -0.5),
        'conv_w_out': nrm(ks[17], (DEPTH, CONV_WIDTH, D_MODEL), CONV_WIDTH ** -0.5),
        's5_a_re': s5_a_re,
        's5_a_im': s5_a_im,
        's5_log_dt': s5_log_dt,
        's5_b_re': nrm(ks[18], (DEPTH, 2, S5_GROUPS, S5_STATE, S5_GROUP), (2 * S5_GROUP) ** -0.5),
        's5_b_im': nrm(ks[19], (DEPTH, 2, S5_GROUPS, S5_STATE, S5_GROUP), (2 * S5_GROUP) ** -0.5),
        's5_c_re': nrm(ks[20], (DEPTH, 2, S5_GROUPS, S5_GROUP, S5_STATE), (2 * S5_STATE) ** -0.5),
        's5_c_im': nrm(ks[21], (DEPTH, 2, S5_GROUPS, S5_GROUP, S5_STATE), (2 * S5_STATE) ** -0.5),
        's5_d': nrm(ks[22], (DEPTH, S5_WIDTH), 1.0),
        's5_w_glu': nrm(ks[23], (DEPTH, S5_WIDTH, 2 * D_MODEL), S5_WIDTH ** -0.5),
        'w_out': nrm(ks[24], (DEPTH, D_MODEL, D_MODEL), D_MODEL ** -0.5),
        'norm_mlp': gain(ks[25], (DEPTH, D_MODEL)),
        'mlp_w1': nrm(ks[26], (DEPTH, D_MODEL, D_FF), D_MODEL ** -0.5),
        'mlp_w2': nrm(ks[27], (DEPTH, D_FF, D_MODEL), D_FF ** -0.5),
        'norm_final': gain(ks[28], (D_MODEL,)),
    }


def reference(x, c, ctx, c_ctx, ada_w, ada_b, norm_mix, w_in, mla_q_norm, mla_w_uq, mla_kv_norm, mla_w_ukv,
              mla_w_o, conv_w, conv_w_out, s5_a_re, s5_a_im, s5_log_dt, s5_b_re, s5_b_im, s5_c_re, s5_c_im,
              s5_d, s5_w_glu, w_out, norm_mlp, mlp_w1, mlp_w2, norm_final):
    n_tokens = x.shape[1]
    rope = axial_rope_tables(n_tokens, x.dtype)
    silu_c = jax.nn.silu(c)
    silu_cc = jax.nn.silu(c_ctx)
    h = ctx
    for i in range(DEPTH):
        need_ctx_out = i < DEPTH - 1
        mod_x = jnp.split((silu_c @ ada_w[i] + ada_b[i])[:, None, :], 6, axis=-1)
        mod_h = jnp.split((silu_cc @ ada_w[i] + ada_b[i])[None, None, :], 6, axis=-1)
        xn = modulate(rms_norm(x, norm_mix[i]), mod_x[0], mod_x[1])
        hn = modulate(rms_norm(h, norm_mix[i]), mod_h[0], mod_h[1])
        y_x, y_h = token_mixers(xn, hn, w_in[i], mla_q_norm[i], mla_w_uq[i], mla_kv_norm[i], mla_w_ukv[i],
                                mla_w_o[i], conv_w[i], conv_w_out[i], s5_a_re[i], s5_a_im[i], s5_log_dt[i],
                                s5_b_re[i], s5_b_im[i], s5_c_re[i], s5_c_im[i], s5_d[i], s5_w_glu[i], w_out[i],
                                rope, need_ctx_out)
        x = x + mod_x[2] * y_x
        x = x + mod_x[5] * squared_relu_mlp(modulate(rms_norm(x, norm_mlp[i]), mod_x[3], mod_x[4]),
                                            mlp_w1[i], mlp_w2[i])
        if need_ctx_out:
            h = h + mod_h[2] * y_h
            h = h + mod_h[5] * squared_relu_mlp(modulate(rms_norm(h, norm_mlp[i]), mod_h[3], mod_h[4]),
                                                mlp_w1[i], mlp_w2[i])
    return rms_norm(x, norm_final)
```

```python
import math
from contextlib import ExitStack

import numpy as np
import concourse.bass as bass
import concourse.mybir as mybir
from concourse.bass_utils import run_bass_kernel_spmd

F32 = mybir.dt.float32
BF16 = mybir.dt.bfloat16
F32R = mybir.dt.float32r
U8 = mybir.dt.uint8
ALU = mybir.AluOpType
AF = mybir.ActivationFunctionType
AX = mybir.AxisListType

D = 1024
KC = 8
SEQ = 4096
CTX = 256
NTOK = SEQ + CTX
DEPTH = 2
IN_COLS = 5664
OFF_Q, OFF_KV, OFF_PE, OFF_CB, OFF_CC, OFF_CX, OFF_S5, OFF_G = 0, 256, 512, 544, 1056, 1568, 2080, 2592
EPS = 1e-6
ATTN_SCALE = 1.0 / math.sqrt(96.0)
NHEAD = 8
TWO_PI = 2.0 * math.pi
MAGIC = 12582912.0

DSIZE = {F32: 4, BF16: 2, F32R: 4, U8: 1}


class Obj:
    __slots__ = ("name", "w", "r_eng", "r_dma", "excl")

    def __init__(self, name="", excl=False):
        self.name = name
        self.excl = excl
        self.w = None
        self.r_eng = {}
        self.r_dma = []


class Sched:
    CENG = ("pe", "act", "dve", "pool")
    QENG = ("sp", "pool", "act")

    def __init__(self, nc, nd=14):
        self.nc = nc
        self.nd = nd
        self.ops = {e: [] for e in ("pe", "act", "dve", "pool", "sp")}
        self.cnt = {e: 0 for e in self.CENG}
        self.seen = {e: {} for e in self.ops}
        self.dma_n = {q: 0 for q in self.QENG}
        self.sem = {}
        self.dsem = {}
        self.base_reads = None

    def _need(self, eng, ev, waits):
        key, val = ev
        if self.seen[eng].get(key, 0) < val:
            self.seen[eng][key] = val
            waits.append(ev)

    def _deps(self, eng, reads, writes, waits):
        for o in reads:
            if o.w is not None:
                self._need(eng, o.w, waits)
            if o.excl:
                for e2, idx in o.r_eng.items():
                    if e2 != eng:
                        self._need(eng, (("E", e2), idx), waits)
        for o in writes:
            if o.w is not None:
                if not (eng == "pe" and o.w[0] == ("E", "pe")):
                    self._need(eng, o.w, waits)
            for e2, idx in o.r_eng.items():
                self._need(eng, (("E", e2), idx), waits)
            for ev in o.r_dma:
                self._need(eng, ev, waits)

    def op(self, eng, fn, reads=(), writes=()):
        waits = []
        self._deps(eng, reads, writes, waits)
        self.cnt[eng] += 1
        me = (("E", eng), self.cnt[eng])
        for o in writes:
            o.w = me
            o.r_eng = {}
            o.r_dma = []
        for o in reads:
            if o.w is not me:
                o.r_eng[eng] = self.cnt[eng]
        self.ops[eng].append((waits, fn, None))

    def dma(self, q, out, in_, reads=(), writes=(), **kw):
        waits = []
        self._deps(q, reads, writes, waits)
        k = self.dma_n[q]
        self.dma_n[q] += 1
        si = k % self.nd
        val = 16 * (k // self.nd + 1)
        key = ("D", q, si)
        if k >= self.nd:
            self._need(q, (key, val - 16), waits)
        me = (key, val)
        for o in writes:
            o.w = me
            o.r_eng = {}
            o.r_dma = []
        for o in reads:
            o.r_dma.append(me)

        def fn(h, out=out, in_=in_, kw=kw):
            return h.dma_start(out=out, in_=in_, **kw)

        self.ops[q].append((waits, fn, key))
        if q in self.cnt:
            pass

    def barrier(self):
        evs = []
        for e in self.CENG:
            if self.cnt[e] > 0:
                evs.append((("E", e), self.cnt[e]))
        for q in self.QENG:
            n = self.dma_n[q]
            for si in range(min(n, self.nd)):
                uses = (n - si + self.nd - 1) // self.nd
                evs.append((("D", q, si), 16 * uses))
        for e in self.ops:
            waits = []
            for ev in evs:
                self._need(e, ev, waits)
            if waits:
                self.ops[e].append((waits, None, None))

    def emit(self, stack):
        nc = self.nc
        for e in self.CENG:
            self.sem[e] = stack.enter_context(nc.semaphore("s_" + e))
        for q in self.QENG:
            for si in range(min(self.dma_n[q], self.nd)):
                self.dsem[(q, si)] = stack.enter_context(nc.semaphore("d_%s_%d" % (q, si)))
        block = stack.enter_context(nc.Block())

        def semof(key):
            if key[0] == "E":
                return self.sem[key[1]]
            return self.dsem[(key[1], key[2])]

        def run(ename, h):
            for waits, fn, inc in self.ops[ename]:
                for key, val in waits:
                    h.wait_ge(semof(key), val)
                if fn is None:
                    continue
                ins = fn(h)
                if inc is None:
                    ins.then_inc(self.sem[ename], 1)
                else:
                    ins.then_inc(self.dsem[(inc[1], inc[2])], 16)

        @block.tensor
        def _(h):
            run("pe", h)

        @block.scalar
        def _(h):
            run("act", h)

        @block.vector
        def _(h):
            run("dve", h)

        @block.gpsimd
        def _(h):
            run("pool", h)

        @block.sync
        def _(h):
            run("sp", h)


class Arena:
    def __init__(self, ap_bytes, nbytes):
        self.base = ap_bytes
        self.nbytes = nbytes
        self.off = 0
        self.mark_ = 0

    def alloc(self, free_shape, dtype, parts=128):
        n = 1
        for s in free_shape:
            n *= s
        nb = n * DSIZE[dtype]
        off = (self.off + 63) // 64 * 64
        if off + nb > self.nbytes:
            raise RuntimeError("SBUF arena overflow: need %d at %d of %d" % (nb, off, self.nbytes))
        self.off = off + nb
        v = self.base[:, off:off + nb].bitcast(dtype)
        if len(free_shape) > 1:
            names = " ".join("d%d" % i for i in range(len(free_shape)))
            kw = {"d%d" % i: free_shape[i] for i in range(len(free_shape))}
            v = v.rearrange("p (%s) -> p %s" % (names, names), **kw)
        return v

    def mark(self):
        return self.off

    def reset(self, m):
        self.off = m


def bc(ap, shape):
    return ap.to_broadcast(list(shape))


class Builder:
    def __init__(self, stages=None, debug=False):
        self.stages = stages
        self.debug = debug
        self.nc = bass.Bass("TRN2", target_bir_lowering=False)
        self.S = Sched(self.nc)
        self.dram = {}
        self.objs = {}

    def din(self, name, shape, dtype=F32):
        t = self.nc.dram_tensor(name, list(shape), dtype, kind="ExternalInput").ap()
        self.dram[name] = t
        return t

    def dout(self, name, shape, dtype=F32):
        t = self.nc.dram_tensor(name, list(shape), dtype, kind="ExternalOutput").ap()
        self.dram[name] = t
        return t

    def dscr(self, name, shape, dtype=F32):
        if self.debug:
            t = self.nc.dram_tensor(name, list(shape), dtype, kind="ExternalOutput").ap()
        else:
            t = self.nc.dram_tensor(name, list(shape), dtype).ap()
        self.dram[name] = t
        return t

    def O(self, *key):
        if key not in self.objs:
            self.objs[key] = Obj(str(key))
        return self.objs[key]

    def mm(self, out, lhsT, rhs, start, stop, reads, writes, **kw):
        self.S.op("pe", lambda h: h.matmul(out, lhsT, rhs, start=start, stop=stop, **kw), reads, writes)

    def act(self, out, in_, func, reads, writes, bias=None, scale=None):
        kw = {}
        if bias is not None:
            kw["bias"] = bias
        if scale is not None:
            kw["scale"] = scale
        self.S.op("act", lambda h: h.activation(out=out, in_=in_, func=func, **kw), reads, writes)

    def tt(self, eng, out, in0, in1, op, reads, writes):
        self.S.op(eng, lambda h: h.tensor_tensor(out, in0, in1, op), reads, writes)

    def ts(self, eng, out, in0, s1, s2, op0, op1, reads, writes):
        if op1 is None:
            self.S.op(eng, lambda h: h.tensor_scalar(out, in0, s1, None, op0), reads, writes)
        else:
            self.S.op(eng, lambda h: h.tensor_scalar(out, in0, s1, s2, op0, op1), reads, writes)

    def stt(self, out, in0, scalar, in1, op0, op1, reads, writes):
        self.S.op("dve", lambda h: h.scalar_tensor_tensor(out, in0, scalar, in1, op0, op1), reads, writes)

    def cp(self, eng, out, in_, reads, writes):
        if eng == "act":
            self.S.op("act", lambda h: h.copy(out, in_), reads, writes)
        else:
            self.S.op(eng, lambda h: h.tensor_copy(out, in_), reads, writes)

    def memset(self, eng, ap, val, writes):
        self.S.op(eng, lambda h: h.memset(ap, val), (), writes)

    def recip(self, out, in_, reads, writes):
        self.S.op("dve", lambda h: h.reciprocal(out, in_), reads, writes)

    def dma(self, q, out, in_, reads, writes, **kw):
        self.S.dma(q, out, in_, reads, writes, **kw)


    def declare(self):
        din, dscr = self.din, self.dscr
        self.xT = din("xT", [D, SEQ])
        self.ctxT = din("ctxT", [D, CTX])
        self.cc = din("cc", [128, KC, 2])
        self.ada_w = din("ada_w", [DEPTH, D, 6 * D])
        self.ada_b2 = din("ada_b2", [DEPTH, 128, 48])
        self.nmix = din("nmix", [DEPTH, 128, KC])
        self.nmlp = din("nmlp", [DEPTH, 128, KC])
        self.nfin = din("nfin", [128, KC])
        self.w_in = din("w_in", [DEPTH, D, IN_COLS])
        self.qn = din("qn", [DEPTH, 128, 2])
        self.kvn = din("kvn", [DEPTH, 128, 2])
        self.w_uq = din("w_uq", [DEPTH, 256, 768])
        self.w_ukv = din("w_ukv", [DEPTH, 256, 1024])
        self.w_o = din("w_o", [DEPTH, 512, D])
        self.conv_w2 = din("conv_w2", [DEPTH, 128, 4, 3])
        self.conv_w_out = din("conv_w_out", [DEPTH, 512, D])
        self.s5_are = din("s5_are", [DEPTH, 128, 64])
        self.s5_aim = din("s5_aim", [DEPTH, 128, 64])
        self.s5_ldt = din("s5_ldt", [DEPTH, 128, 64])
        self.s5_bre = din("s5_bre", [DEPTH, 128, 64, 16])
        self.s5_bim = din("s5_bim", [DEPTH, 128, 64, 16])
        self.s5_cre = din("s5_cre", [DEPTH, 128, 64, 16])
        self.s5_cim = din("s5_cim", [DEPTH, 128, 64, 16])
        self.s5_dv = din("s5_dv", [DEPTH, 128, 512])
        self.w_glu = din("w_glu", [DEPTH, 512, 2 * D])
        self.w_out = din("w_out", [DEPTH, D, D])
        self.mlp_w1 = din("mlp_w1", [DEPTH, D, 4 * D])
        self.mlp_w2 = din("mlp_w2", [DEPTH, 4 * D, D])
        self.c_ident = din("c_ident", [128, 128])
        self.c_jperm = din("c_jperm", [128, 128])
        self.c_maskf = din("c_maskf", [128, 128])
        self.c_maskb = din("c_maskb", [128, 128])
        self.c_selc = din("c_selc", [128, 2 * 8 * 128])
        self.c_selc2 = din("c_selc2", [128, 2 * 8 * 128])
        self.c_self = din("c_self", [128, 64 * 128])
        self.c_ropec = din("c_ropec", [32, SEQ])
        self.c_ropes = din("c_ropes", [32, SEQ])
        self.outT = self.dout("outT", [D, SEQ])
        self.XA = dscr("XA", [D, SEQ])
        self.HA = dscr("HA", [D, CTX])
        self.X1 = dscr("X1", [D, SEQ])
        self.H1 = dscr("H1", [D, CTX])
        self.ZQN = dscr("ZQN", [256, NTOK], BF16)
        self.ZKVN = dscr("ZKVN", [256, NTOK], BF16)
        self.KPE = dscr("KPE", [32, NTOK], BF16)
        self.ZS5 = dscr("ZS5", [512, NTOK], BF16)
        self.UCV = dscr("UCV", [512, NTOK], BF16)
        self.ATTO = dscr("ATTO", [512, NTOK], BF16)
        self.GS5 = dscr("GS5", [512, NTOK], BF16)

    def setup_onchip(self, stack):
        nc = self.nc
        ARENA = 177 * 1024
        self.arena_t = stack.enter_context(nc.sbuf_tensor("arena", [128, ARENA], U8))
        self.A = Arena(self.arena_t[:], ARENA)
        FRW = 2 * 514 * 2 + 2 * 36 * 2 + 4 * 1280 + 128
        self.fr_t = stack.enter_context(nc.sbuf_tensor("fr32r", [128, FRW], F32R))
        self.FR = self.fr_t[:]
        self.PS = []
        self.PS2 = []
        for i in range(4):
            t = stack.enter_context(nc.psum_tensor("ps%d" % i, [128, 1024], F32))
            self.PS2.append(t[:])
            self.PS.append(t[:, 0:512])
            self.PS.append(t[:, 512:1024])
        self.PSO = [self.O("psum", i) for i in range(8)]
        for o in self.PSO:
            o.excl = True
        A = self.A
        self.IDENTB = A.alloc([128], BF16)
        self.ONESB = A.alloc([128], BF16)
        self.MODS = A.alloc([DEPTH, 6, KC, 2], F32)
        self.CONST = self.O("const")
        self.MODSO = self.O("mods")
        c = [self.CONST]
        self.dma("pool", self.IDENTB, self.c_ident, (), c)
        self.memset("dve", self.ONESB, 1.0, c)
        self.pmark = A.mark()

    def tiles(self, n):
        res = []
        for o in range(0, CTX, n):
            res.append((True, o, o, min(n, CTX - o)))
        for o in range(0, SEQ, n):
            res.append((False, CTX + o, o, min(n, SEQ - o)))
        return res

    def xsrc(self, layer, which, is_ctx):
        if which == "in":
            if layer == 0:
                return self.ctxT if is_ctx else self.xT
            return self.HA if is_ctx else self.XA
        if which == "mid":
            return self.H1 if is_ctx else self.X1
        if which == "out":
            if layer == DEPTH - 1:
                return None if is_ctx else self.outT
            return self.HA if is_ctx else self.XA
        raise ValueError(which)

    def phase0(self):
        A, S = self.A, self.S
        m0 = A.mark()
        CC = A.alloc([KC, 2], F32)
        SC = A.alloc([KC, 2], F32)
        ADW = [A.alloc([6 * D], BF16) for _ in range(2)]
        SCB = A.alloc([KC, 2], BF16)
        ACC = A.alloc([48, 2], F32)
        ADB = A.alloc([48], F32)
        NMX = A.alloc([KC], F32)
        NML = A.alloc([KC], F32)
        MODV = A.alloc([48, 2], F32)
        oCC, oSC, oACC, oADB, oNM, oMODV = (self.O("p0", n) for n in ("cc", "sc", "acc", "adb", "nm", "modv"))
        oADW = [self.O("p0", "adw", i) for i in range(2)]
        ps, pso = self.PS[0], self.PSO[0]
        self.dma("sp", CC, self.cc, (), [oCC])
        self.act(SC, CC, AF.Silu, [oCC], [oSC])
        self.cp("dve", SCB, SC, [oSC], [oSC])
        for l in range(DEPTH):
            self.dma("sp", ADB, self.ada_b2[l], (), [oADB])
            self.dma("sp", NMX, self.nmix[l], (), [oNM])
            self.dma("sp", NML, self.nmlp[l], (), [oNM])
            for kc in range(KC):
                b = kc % 2
                half = 3 * D
                for c0 in range(0, 6 * D, 2048):
                    self.dma("pool", ADW[b][:, c0:c0 + 2048], self.ada_w[l, kc * 128:(kc + 1) * 128, c0:c0 + 2048], (), [oADW[b]])
                for ft in range(48):
                    self.mm(ps[:, 2 * ft:2 * ft + 2], ADW[b][:, ft * 128:(ft + 1) * 128], SCB[:, kc, :],
                            True, True, [oADW[b], oSC], [pso])
                accv = ACC.rearrange("p a b -> p (a b)")
                if kc == 0:
                    self.cp("dve", accv, ps[:, 0:96], [pso], [oACC])
                else:
                    self.tt("dve", accv, ps[:, 0:96], accv, ALU.add, [pso, oACC], [oACC])
            self.tt("dve", MODV, ACC, bc(ADB.unsqueeze(2), [128, 48, 2]), ALU.add, [oACC, oADB], [oMODV])
            mv = MODV.rearrange("p (j k) v -> p j k v", j=6)
            M = self.MODS
            self.stt(M[:, l, 0], mv[:, 1], 1.0, bc(NMX.unsqueeze(2), [128, KC, 2]), ALU.add, ALU.mult,
                     [oMODV, oNM], [self.MODSO])
            self.cp("dve", M[:, l, 1], mv[:, 0], [oMODV], [self.MODSO])
            self.cp("dve", M[:, l, 2], mv[:, 2], [oMODV], [self.MODSO])
            self.stt(M[:, l, 3], mv[:, 4], 1.0, bc(NML.unsqueeze(2), [128, KC, 2]), ALU.add, ALU.mult,
                     [oMODV, oNM], [self.MODSO])
            self.cp("dve", M[:, l, 4], mv[:, 3], [oMODV], [self.MODSO])
            self.cp("dve", M[:, l, 5], mv[:, 5], [oMODV], [self.MODSO])
        S.barrier()
        A.reset(m0)

    def mod(self, layer, kind, kc, is_ctx):
        v = 1 if is_ctx else 0
        return self.MODS[:, layer, kind, kc, v:v + 1]

    def norm_alloc(self, N, nbuf=1):
        A = self.A
        d = {}
        d["XSQ"] = A.alloc([KC, N], BF16)
        d["R0"] = A.alloc([N], F32)
        d["RSTD"] = A.alloc([N], F32)
        d["TMP"] = [A.alloc([N], F32) for _ in range(2)]
        d["XN"] = [A.alloc([KC, N], BF16) for _ in range(nbuf)]
        d["oXSQ"] = Obj()
        d["oR0"] = Obj()
        d["oRSTD"] = Obj()
        d["oTMP"] = [Obj(), Obj()]
        d["oXN"] = [Obj() for _ in range(nbuf)]
        d["i"] = 0
        return d

    def norm_a(self, nb, XT, oXT, N):
        self.tt("pool", nb["XSQ"][:, :, :N], XT[:, :, :N], XT[:, :, :N], ALU.mult, [oXT], [nb["oXSQ"]])

    def norm_tile(self, nb, XT, oXT, N, layer, kA, kB, is_ctx, psi=0, ddim=float(D)):
        self.norm_a(nb, XT, oXT, N)
        return self.norm_b(nb, XT, oXT, N, layer, kA, kB, is_ctx, psi, ddim)

    def norm_b(self, nb, XT, oXT, N, layer, kA, kB, is_ctx, psi=0, ddim=float(D)):
        i = nb["i"]
        nb["i"] += 1
        XN = nb["XN"][i % len(nb["XN"])]
        oXN = nb["oXN"][i % len(nb["XN"])]
        XSQ, R0, RSTD = nb["XSQ"], nb["R0"], nb["RSTD"]
        ps, pso = self.PS[psi], self.PSO[psi]
        for kc in range(KC):
            self.mm(ps[:, :N], self.ONESB, XSQ[:, kc, :N], kc == 0, kc == KC - 1, [nb["oXSQ"], self.CONST], [pso])
        self.act(R0[:, :N], ps[:, :N], AF.Sqrt, [pso], [nb["oR0"]], bias=EPS, scale=1.0 / ddim)
        self.recip(RSTD[:, :N], R0[:, :N], [nb["oR0"]], [nb["oRSTD"]])
        for kc in range(KC):
            t = kc % 2
            self.stt(nb["TMP"][t][:, :N], XT[:, kc, :N], self.mod(layer, kA, kc, is_ctx), RSTD[:, :N],
                     ALU.mult, ALU.mult, [oXT, nb["oRSTD"], self.MODSO], [nb["oTMP"][t]])
            self.act(XN[:, kc, :N], nb["TMP"][t][:, :N], AF.Identity, [nb["oTMP"][t], self.MODSO], [oXN],
                     bias=self.mod(layer, kB, kc, is_ctx))
        return XN, oXN

    def load_w(self, dst, src_rows_ap, K, reads_obj, q="pool", maxcols=2048):
        C = src_rows_ap.shape[1]
        for kc in range(K):
            for c0 in range(0, C, maxcols):
                c1 = min(C, c0 + maxcols)
                self.dma(q, dst[:, kc, c0:c1], src_rows_ap[kc * 128:(kc + 1) * 128, c0:c1], (), [reads_obj])

    def phase1(self, layer):
        A, S = self.A, self.S
        last = layer == DEPTH - 1
        m0 = A.mark()
        N = 512
        WA = A.alloc([KC, 2080], BF16)
        WPR = A.alloc([KC, 96], BF16)
        oW = Obj()
        oWPR = Obj()
        QN = A.alloc([2], F32)
        KVN = A.alloc([2], F32)
        oQN = Obj()
        XT = [A.alloc([KC, N], F32) for _ in range(2)]
        oXT = [Obj(), Obj()]
        nb = self.norm_alloc(N, 2)
        ZRAW = A.alloc([2, N], F32)
        ZSQ = A.alloc([2, N], BF16)
        RZ0 = A.alloc([N], F32)
        RZ = A.alloc([N], F32)
        oZRAW, oZSQ, oRZ0, oRZ = Obj(), Obj(), Obj(), Obj()
        ZQo = [A.alloc([2, N], BF16) for _ in range(2)]
        ZKo = [A.alloc([2, N], BF16) for _ in range(2)]
        oZQo = [Obj(), Obj()]
        oZKo = [Obj(), Obj()]
        KPo = [A.alloc([N], BF16) for _ in range(2)]
        oKPo = [Obj(), Obj()]
        T1 = A.alloc([N], F32)
        T2 = A.alloc([N], F32)
        oT1, oT2 = Obj(), Obj()
        COS = [A.alloc([N], F32) for _ in range(2)]
        SIN = [A.alloc([N], F32) for _ in range(2)]
        oROPE = [Obj(), Obj()]
        CCs = [A.alloc([N], F32) for _ in range(2)]
        oCCs = [Obj(), Obj()]
        Uo = [A.alloc([4, N], BF16) for _ in range(2)]
        oUo = [Obj(), Obj()]
        Z5o = [A.alloc([4, N], BF16) for _ in range(2)]
        oZ5o = [Obj(), Obj()]

        win = self.w_in[layer]
        for kc in range(KC):
            self.dma("pool", WA[:, kc, 0:544], win[kc * 128:(kc + 1) * 128, 0:544], (), [oW])
            self.dma("pool", WA[:, kc, 544:2080], win[kc * 128:(kc + 1) * 128, 1056:2592], (), [oW])
        self.dma("sp", QN, self.qn[layer], (), [oQN])
        self.dma("sp", KVN, self.kvn[layer], (), [oQN])
        self.memset("dve", WPR[:, :, 0:64], 0.0, [oWPR])
        for a in range(2):
            src_hi = WA[:, :, 512 + a * 16 + 8:512 + a * 16 + 16]
            src_lo = WA[:, :, 512 + a * 16:512 + a * 16 + 8]
            self.ts("dve", WPR[:, :, 64 + a * 16:64 + a * 16 + 8], src_hi, -1.0, None, ALU.mult, None, [oW], [oWPR])
            self.cp("dve", WPR[:, :, 64 + a * 16 + 8:64 + a * 16 + 16], src_lo, [oW], [oWPR])

        bank = [2]

        def nbank():
            b = bank[0]
            bank[0] = 2 + (b - 2 + 1) % 6
            return b

        tl = self.tiles(N)
        import os
        parts = os.environ.get("P1PARTS", "q,kv,pe,conv,s5").split(",")
        tl = tl[:int(os.environ.get("P1TILES", "99"))]
        def prep_a(ti):
            is_ctx, dcol, scol, n = tl[ti]
            b2 = ti % 2
            src = self.xsrc(layer, "in", is_ctx)
            xt, oxt = XT[b2], oXT[b2]
            self.dma("sp", xt[:, :, :n], src[:, scol:scol + n].rearrange("(k p) n -> p k n", p=128), (), [oxt])
            if not is_ctx:
                self.dma("sp", COS[b2][64:96, :n], self.c_ropec[:, scol:scol + n], (), [oROPE[b2]])
                self.dma("sp", SIN[b2][64:96, :n], self.c_ropes[:, scol:scol + n], (), [oROPE[b2]])
            self.norm_a(nb, xt, oxt, n)

        def prep(ti):
            is_ctx, dcol, scol, n = tl[ti]
            return self.norm_b(nb, XT[ti % 2], oXT[ti % 2], n, layer, 0, 1, is_ctx)

        prep_a(0)
        nxt = prep(0)
        for ti, (is_ctx, dcol, scol, n) in enumerate(tl):
            b2 = ti % 2
            XN, oXN = nxt
            if ti + 1 < len(tl):
                prep_a(ti + 1)

            def proj(c0, M, W=WA, oWW=oW):
                b = nbank()
                for kc in range(KC):
                    self.mm(self.PS[b][:M, :n], W[:, kc, c0:c0 + M], XN[:, kc, :n], kc == 0, kc == KC - 1,
                            [oWW, oXN], [self.PSO[b]])
                return b

            def znorm(c0, gain, outb, ooutb, dst):
                bs = [proj(c0, 128), proj(c0 + 128, 128)]
                for j, b in enumerate(bs):
                    self.act(ZSQ[:, j, :n], self.PS[b][:, :n], AF.Square, [self.PSO[b]], [oZSQ])
                    self.cp("dve", ZRAW[:, j, :n], self.PS[b][:, :n], [self.PSO[b]], [oZRAW])
                for j in range(2):
                    self.mm(self.PS[1][:, :n], self.ONESB, ZSQ[:, j, :n], j == 0, j == 1, [oZSQ, self.CONST], [self.PSO[1]])
                self.act(RZ0[:, :n], self.PS[1][:, :n], AF.Sqrt, [self.PSO[1]], [oRZ0], bias=EPS, scale=1.0 / 256.0)
                self.recip(RZ[:, :n], RZ0[:, :n], [oRZ0], [oRZ])
                for j in range(2):
                    self.stt(outb[:, j, :n], ZRAW[:, j, :n], gain[:, j:j + 1], RZ[:, :n], ALU.mult, ALU.mult,
                             [oZRAW, oRZ, oQN], [ooutb])
                self.dma("sp", dst[:, dcol:dcol + n].rearrange("(j p) n -> p j n", p=128), outb[:, :, :n], [ooutb], [])

            need_q = not (is_ctx and last)
            if need_q and "q" in parts:
                znorm(0, QN, ZQo[b2], oZQo[b2], self.ZQN)
            if "kv" in parts:
                znorm(256, KVN, ZKo[b2], oZKo[b2], self.ZKVN)
            if "pe" not in parts:
                pass
            else:
              bpe = proj(448, 96)
              if is_ctx:
                self.cp("act", KPo[b2][64:96, :n], self.PS[bpe][64:96, :n], [self.PSO[bpe]], [oKPo[b2]])
              else:
                brot = proj(0, 96, WPR, oWPR)
                self.tt("dve", T1[64:96, :n], self.PS[bpe][64:96, :n], COS[b2][64:96, :n], ALU.mult,
                        [self.PSO[bpe], oROPE[b2]], [oT1])
                self.tt("dve", T2[64:96, :n], self.PS[brot][64:96, :n], SIN[b2][64:96, :n], ALU.mult,
                        [self.PSO[brot], oROPE[b2]], [oT2])
                self.tt("dve", KPo[b2][64:96, :n], T1[64:96, :n], T2[64:96, :n], ALU.add, [oT1, oT2], [oKPo[b2]])
              self.dma("sp", self.KPE[:, dcol:dcol + n], KPo[b2][64:96, :n], [oKPo[b2]], [])
            if not (is_ctx and last) and "conv" in parts:
                for j in range(4):
                    bcc = proj(544 + j * 128, 128)
                    bcx = proj(544 + 512 + j * 128, 128)
                    cs, ocs = CCs[j % 2], oCCs[j % 2]
                    self.cp("act", cs[:, :n], self.PS[bcc][:, :n], [self.PSO[bcc]], [ocs])
                    self.tt("dve", Uo[b2][:, j, :n], self.PS[bcx][:, :n], cs[:, :n], ALU.mult, [self.PSO[bcx], ocs], [oUo[b2]])
                self.dma("sp", self.UCV[:, dcol:dcol + n].rearrange("(j p) n -> p j n", p=128), Uo[b2][:, :, :n], [oUo[b2]], [])
            if ti + 1 < len(tl):
                nxt = prep(ti + 1)
            for j in range(4 if "s5" in parts else 0):
                b5 = proj(544 + 1024 + j * 128, 128)
                zo = Z5o[b2][:, j, :n].rearrange("p (s k) -> p s k", s=8)
                zi = self.PS[b5][:, :n].rearrange("p (k s) -> p s k", s=8)
                self.cp("act" if j % 2 == 0 else "dve", zo, zi, [self.PSO[b5]], [oZ5o[b2]])
            if "s5" in parts:
                self.dma("sp", self.ZS5[:, dcol:dcol + n].rearrange("(j p) n -> p j n", p=128), Z5o[b2][:, :, :n], [oZ5o[b2]], [])
        S.barrier()
        A.reset(m0)

    def phase4(self, layer):
        A, S = self.A, self.S
        last = layer == DEPTH - 1
        m0 = A.mark()
        N = 256
        WB = A.alloc([KC, 3584], BF16)
        CWO = A.alloc([4, D], BF16)
        WO = A.alloc([4, D], BF16)
        WGLU = A.alloc([4, 2 * D], BF16)
        WOUT = A.alloc([KC, D], BF16)
        CW = A.alloc([4, 3], F32)
        oW = Obj()
        XT = [A.alloc([KC, N], F32) for _ in range(2)]
        oXT = [Obj(), Obj()]
        nb = self.norm_alloc(N, 2)
        UH = [A.alloc([4, N + 2], BF16) for _ in range(2)]
        AT = [A.alloc([4, N], BF16) for _ in range(2)]
        G5 = [A.alloc([4, N], BF16) for _ in range(2)]
        oIN = [Obj(), Obj()]
        CT = A.alloc([N], F32)
        oCT = Obj()
        CONVG = A.alloc([4, N], BF16)
        oCONVG = Obj()
        SGs = [[A.alloc([N], F32) for _ in range(4)] for _ in range(2)]
        oSGs = [[Obj() for _ in range(4)] for _ in range(2)]
        Ms = [[A.alloc([N], F32) for _ in range(3)] for _ in range(2)]
        oMs = [[Obj() for _ in range(3)] for _ in range(2)]
        MRG = A.alloc([KC, N], BF16)
        oMRG = Obj()

        win = self.w_in[layer]
        for kc in range(KC):
            self.dma("pool", WB[:, kc, 0:512], win[kc * 128:(kc + 1) * 128, 544:1056], (), [oW])
            self.dma("pool", WB[:, kc, 512:2048], win[kc * 128:(kc + 1) * 128, 2592:4128], (), [oW])
            self.dma("pool", WB[:, kc, 2048:3584], win[kc * 128:(kc + 1) * 128, 4128:5664], (), [oW])
        self.load_w(CWO, self.conv_w_out[layer], 4, oW)
        self.load_w(WO, self.w_o[layer], 4, oW)
        self.load_w(WGLU, self.w_glu[layer], 4, oW)
        self.load_w(WOUT, self.w_out[layer], KC, oW)
        self.dma("sp", CW, self.conv_w2[layer], (), [oW])

        bank = [1]

        def nbank():
            b = bank[0]
            bank[0] = 1 + (b - 1 + 1) % 7
            return b

        tl = [t for t in self.tiles(N) if not (t[0] and last)]
        def prep_a(ti):
            is_ctx, dcol, scol, n = tl[ti]
            b2 = ti % 2
            src = self.xsrc(layer, "in", is_ctx)
            seqlen = CTX if is_ctx else SEQ
            xt, oxt = XT[b2], oXT[b2]
            uh, at, g5, oin = UH[b2], AT[b2], G5[b2], oIN[b2]
            self.dma("sp", xt[:, :, :n], src[:, scol:scol + n].rearrange("(k p) n -> p k n", p=128), (), [oxt])
            lo = 1 if scol == 0 else 0
            hi = 1 if scol + n == seqlen else 0
            if lo:
                self.memset("pool", uh[:, :, 0:1], 0.0, [oin])
            if hi:
                self.memset("pool", uh[:, :, n + 1:n + 2], 0.0, [oin])
            self.dma("sp", uh[:, :, lo:n + 2 - hi],
                     self.UCV[:, dcol - 1 + lo:dcol + n + 1 - hi].rearrange("(j p) n -> p j n", p=128), (), [oin])
            self.dma("sp", at[:, :, :n], self.ATTO[:, dcol:dcol + n].rearrange("(j p) n -> p j n", p=128), (), [oin])
            self.dma("sp", g5[:, :, :n], self.GS5[:, dcol:dcol + n].rearrange("(j p) n -> p j n", p=128), (), [oin])
            self.norm_a(nb, xt, oxt, n)

        def prep(ti):
            is_ctx, dcol, scol, n = tl[ti]
            return self.norm_b(nb, XT[ti % 2], oXT[ti % 2], n, layer, 0, 1, is_ctx)

        prep_a(0)
        nxt = prep(0)
        for ti, (is_ctx, dcol, scol, n) in enumerate(tl):
            b2 = ti % 2
            dst = self.xsrc(layer, "mid", is_ctx)
            xt, oxt = XT[b2], oXT[b2]
            uh, at, g5, oin = UH[b2], AT[b2], G5[b2], oIN[b2]
            XN, oXN = nxt
            if ti + 1 < len(tl):
                prep_a(ti + 1)

            def proj(c0):
                b = nbank()
                for kc in range(KC):
                    self.mm(self.PS[b][:, :n], WB[:, kc, c0:c0 + 128], XN[:, kc, :n], kc == 0, kc == KC - 1,
                            [oW, oXN], [self.PSO[b]])
                return b

            def proj4(W, c0, X, oX):
                b = nbank()
                for k in range(4):
                    self.mm(self.PS[b][:, :n], W[:, k, c0:c0 + 128], X[:, k, :n], k == 0, k == 3, [oW, oX], [self.PSO[b]])
                return b

            for j in range(4):
                bcb = proj(j * 128)
                self.ts("dve", CT[:, :n], uh[:, j, 0:n], CW[:, j, 0:1], None, ALU.mult, None, [oin, oW], [oCT])
                self.stt(CT[:, :n], uh[:, j, 1:n + 1], CW[:, j, 1:2], CT[:, :n], ALU.mult, ALU.add, [oin, oW, oCT], [oCT])
                self.stt(CT[:, :n], uh[:, j, 2:n + 2], CW[:, j, 2:3], CT[:, :n], ALU.mult, ALU.add, [oin, oW, oCT], [oCT])
                self.tt("dve", CONVG[:, j, :n], self.PS[bcb][:, :n], CT[:, :n], ALU.mult, [self.PSO[bcb], oCT], [oCONVG])
            for jo in range(8):
                if jo == 4 and ti + 1 < len(tl):
                    nxt = prep(ti + 1)
                bg = [proj(512 + g * 1024 + jo * 128) for g in range(3)]
                batt = proj4(WO, jo * 128, at, oin)
                bcv = proj4(CWO, jo * 128, CONVG, oCONVG)
                bza = proj4(WGLU, jo * 128, g5, oin)
                bzb = proj4(WGLU, D + jo * 128, g5, oin)
                SG, oSG = SGs[jo % 2], oSGs[jo % 2]
                (M1, M2, M3), (oM1, oM2, oM3) = Ms[jo % 2], oMs[jo % 2]
                for g in range(3):
                    self.act(SG[g][:, :n], self.PS[bg[g]][:, :n], AF.Sigmoid, [self.PSO[bg[g]]], [oSG[g]])
                self.act(SG[3][:, :n], self.PS[bzb][:, :n], AF.Sigmoid, [self.PSO[bzb]], [oSG[3]])
                self.tt("dve", M1[:, :n], self.PS[batt][:, :n], SG[0][:, :n], ALU.mult, [self.PSO[batt], oSG[0]], [oM1])
                self.tt("dve", M2[:, :n], self.PS[bcv][:, :n], SG[1][:, :n], ALU.mult, [self.PSO[bcv], oSG[1]], [oM2])
                self.tt("dve", M3[:, :n], self.PS[bza][:, :n], SG[3][:, :n], ALU.mult, [self.PSO[bza], oSG[3]], [oM3])
                self.tt("pool", M1[:, :n], M1[:, :n], M2[:, :n], ALU.add, [oM1, oM2], [oM1])
                self.tt("pool", M3[:, :n], M3[:, :n], SG[2][:, :n], ALU.mult, [oM3, oSG[2]], [oM3])
                self.tt("pool", MRG[:, jo, :n], M1[:, :n], M3[:, :n], ALU.add, [oM1, oM3], [oMRG])
            for jo in range(8):
                b = nbank()
                for k in range(KC):
                    self.mm(self.PS[b][:, :n], WOUT[:, k, jo * 128:(jo + 1) * 128], MRG[:, k, :n], k == 0, k == KC - 1,
                            [oW, oMRG], [self.PSO[b]])
                self.stt(xt[:, jo, :n], self.PS[b][:, :n], self.mod(layer, 2, jo, is_ctx), xt[:, jo, :n],
                         ALU.mult, ALU.add, [self.PSO[b], oxt, self.MODSO], [oxt])
            self.dma("sp", dst[:, scol:scol + n].rearrange("(k p) n -> p k n", p=128), xt[:, :, :n], [oxt], [])
        S.barrier()
        A.reset(m0)

    def phase5(self, layer):
        A, S = self.A, self.S
        last = layer == DEPTH - 1
        m0 = A.mark()
        N = 256
        W1 = A.alloc([KC, 4 * D], BF16)
        W2 = A.alloc([32, D], BF16)
        NF = A.alloc([KC], F32)
        oW = Obj()
        XT = [A.alloc([KC, N], F32) for _ in range(2)]
        oXT = [Obj(), Obj()]
        nb = self.norm_alloc(N, 1)
        SQ = [A.alloc([N], F32) for _ in range(2)]
        oSQ = [Obj(), Obj()]
        H = A.alloc([32, N], BF16)
        oH = Obj()
        self.load_w(W1, self.mlp_w1[layer], KC, oW)
        self.load_w(W2, self.mlp_w2[layer], 32, oW)
        self.dma("sp", NF, self.nfin, (), [oW])
        bank = [1]

        def nbank():
            b = bank[0]
            bank[0] = 1 + (b - 1 + 1) % 7
            return b

        tl = [t for t in self.tiles(N) if not (t[0] and last)]
        def prep_a(ti):
            is_ctx, dcol, scol, n = tl[ti]
            src = self.xsrc(layer, "mid", is_ctx)
            xt, oxt = XT[ti % 2], oXT[ti % 2]
            self.dma("sp", xt[:, :, :n], src[:, scol:scol + n].rearrange("(k p) n -> p k n", p=128), (), [oxt])
            self.norm_a(nb, xt, oxt, n)

        def prep(ti):
            is_ctx, dcol, scol, n = tl[ti]
            return self.norm_b(nb, XT[ti % 2], oXT[ti % 2], n, layer, 3, 4, is_ctx)

        prep_a(0)
        nxt = prep(0)
        for ti, (is_ctx, dcol, scol, n) in enumerate(tl):
            b2 = ti % 2
            dst = self.xsrc(layer, "out", is_ctx)
            xt, oxt = XT[b2], oXT[b2]
            XN, oXN = nxt
            if ti + 1 < len(tl):
                prep_a(ti + 1)
            for m in range(32):
                b = nbank()
                for k in range(KC):
                    self.mm(self.PS[b][:, :n], W1[:, k, m * 128:(m + 1) * 128], XN[:, k, :n], k == 0, k == KC - 1,
                            [oW, oXN], [self.PSO[b]])
                sq, osq = SQ[m % 2], oSQ[m % 2]
                self.act(sq[:, :n], self.PS[b][:, :n], AF.Square, [self.PSO[b]], [osq])
                self.stt(H[:, m, :n], self.PS[b][:, :n], 0.0, sq[:, :n], ALU.is_gt, ALU.mult, [self.PSO[b], osq], [oH])
            if ti + 1 < len(tl):
                nxt = prep(ti + 1)
            for jo in range(8):
                b = nbank()
                for m in range(32):
                    self.mm(self.PS[b][:, :n], W2[:, m, jo * 128:(jo + 1) * 128], H[:, m, :n], m == 0, m == 31,
                            [oW, oH], [self.PSO[b]])
                self.stt(xt[:, jo, :n], self.PS[b][:, :n], self.mod(layer, 5, jo, is_ctx), xt[:, jo, :n],
                         ALU.mult, ALU.add, [self.PSO[b], oxt, self.MODSO], [oxt])
            if not last:
                self.dma("sp", dst[:, scol:scol + n].rearrange("(k p) n -> p k n", p=128), xt[:, :, :n], [oxt], [])
            else:
                XSQ, R0, RSTD = nb["XSQ"], nb["R0"], nb["RSTD"]
                ps, pso = self.PS[0], self.PSO[0]
                self.tt("pool", XSQ[:, :, :n], xt[:, :, :n], xt[:, :, :n], ALU.mult, [oxt], [nb["oXSQ"]])
                for kc in range(KC):
                    self.mm(ps[:, :n], self.ONESB, XSQ[:, kc, :n], kc == 0, kc == KC - 1, [nb["oXSQ"], self.CONST], [pso])
                self.act(R0[:, :n], ps[:, :n], AF.Sqrt, [pso], [nb["oR0"]], bias=EPS, scale=1.0 / D)
                self.recip(RSTD[:, :n], R0[:, :n], [nb["oR0"]], [nb["oRSTD"]])
                for kc in range(KC):
                    self.stt(xt[:, kc, :n], xt[:, kc, :n], NF[:, kc:kc + 1], RSTD[:, :n], ALU.mult, ALU.mult,
                             [oxt, nb["oRSTD"], oW], [oxt])
                self.dma("sp", dst[:, scol:scol + n].rearrange("(k p) n -> p k n", p=128), xt[:, :, :n], [oxt], [])
        S.barrier()
        A.reset(m0)

    def build(self, phases=None, feeds=()):
        self.feeds = set(feeds)
        stack = ExitStack()
        self.stack = stack
        self.declare_all()
        self.setup_onchip(stack)
        if phases is None:
            phases = [("p0", 0)]
            for l in range(DEPTH):
                phases += [("p1", l), ("p2", l), ("p3", l), ("p4", l), ("p5", l)]
        self.S.barrier()
        for name, l in phases:
            if name == "p0":
                self.phase0()
            elif name == "p1":
                self.phase1(l)
            elif name == "p2":
                self.phase2(l)
            elif name == "p3":
                self.phase3(l)
            elif name == "p4":
                self.phase4(l)
            elif name == "p5":
                self.phase5(l)
        self.S.barrier()
        self.S.emit(stack)
        stack.close()
        return self.nc

    def declare_all(self):
        orig = self.dscr

        def dscr(name, shape, dtype=F32):
            if name in self.feeds:
                return self.din(name, shape, dtype)
            return orig(name, shape, dtype)

        self.dscr = dscr
        self.declare()
        self.dscr = orig


def rope_tables():
    rows = SEQ // 64
    row = np.repeat(np.arange(rows), 64)
    col = np.tile(np.arange(64), rows)
    pos = np.stack([row, col], axis=-1).astype(np.float32)
    n_freq = 8
    inv = (np.float32(10000.0) ** (-np.arange(n_freq, dtype=np.float32) / np.float32(n_freq))).astype(np.float32)
    ang = pos[:, :, None, None] * inv[None, None, None, :]
    ang = np.broadcast_to(ang, (SEQ, 2, 2, n_freq)).reshape(SEQ, 32).astype(np.float32)
    return np.ascontiguousarray(np.cos(ang).astype(np.float32).T), np.ascontiguousarray(np.sin(ang).astype(np.float32).T)


def const_inputs():
    c = {}
    c["c_ident"] = np.eye(128, dtype=np.float32)
    j = np.zeros((128, 128), np.float32)
    for n in range(64):
        j[n, n + 64] = 1.0
        j[n + 64, n] = 1.0
    c["c_jperm"] = j
    idx = np.arange(128) // 16
    c["c_maskf"] = (idx[None, :] >= idx[:, None]).astype(np.float32)
    c["c_maskb"] = (idx[:, None] >= idx[None, :]).astype(np.float32)
    sel = np.zeros((128, 2, 8, 128), np.float32)
    for p in range(128):
        r = p % 32
        par, cch = r // 16, r % 16
        for jj in range(8):
            sel[p, par, jj, jj * 16 + cch] = 1.0
    c["c_selc"] = sel.reshape(128, 2048)
    sel2 = sel.copy()
    sel2[:96] = 0.0
    c["c_selc2"] = sel2.reshape(128, 2048)
    sf = np.zeros((128, 8, 8, 128), np.float32)
    for p in range(128):
        a_, cch = p // 16, p % 16
        for b_ in range(8):
            sf[p, a_, b_, b_ * 16 + cch] = 1.0
    c["c_self"] = sf.reshape(128, 64 * 128)
    rc, rs = rope_tables()
    c["c_ropec"] = rc
    c["c_ropes"] = rs
    return c


def fm(v, k):
    v = np.asarray(v, np.float32)
    lead = v.shape[:-1]
    return np.ascontiguousarray(np.swapaxes(v.reshape(lead + (k, 128)), -1, -2))


def shared_inputs(inp):
    f = lambda a: np.ascontiguousarray(np.asarray(a, np.float32))
    s = {}
    s["ada_w"] = f(inp["ada_w"])
    s["ada_b2"] = fm(inp["ada_b"], 48)
    s["nmix"] = fm(inp["norm_mix"], KC)
    s["nmlp"] = fm(inp["norm_mlp"], KC)
    s["nfin"] = fm(inp["norm_final"], KC)
    s["w_in"] = f(inp["w_in"])
    s["qn"] = fm(inp["mla_q_norm"], 2)
    s["kvn"] = fm(inp["mla_kv_norm"], 2)
    s["w_uq"] = f(inp["mla_w_uq"])
    s["w_ukv"] = f(inp["mla_w_ukv"])
    s["w_o"] = f(inp["mla_w_o"])
    cw = np.asarray(inp["conv_w"], np.float32)
    s["conv_w2"] = np.ascontiguousarray(cw.reshape(DEPTH, 3, 4, 128).transpose(0, 3, 2, 1))
    s["conv_w_out"] = f(inp["conv_w_out"])

    def pdup(a):
        a = np.asarray(a, np.float32).reshape(DEPTH, 64, 64).transpose(0, 2, 1)
        return np.ascontiguousarray(np.concatenate([a, a], axis=1))

    s["s5_are"] = pdup(inp["s5_a_re"])
    s["s5_aim"] = pdup(inp["s5_a_im"])
    ldt = np.asarray(inp["s5_log_dt"], np.float32).reshape(DEPTH, 1, 64)
    s["s5_ldt"] = np.ascontiguousarray(np.broadcast_to(ldt, (DEPTH, 128, 64)))

    def bdup(a):
        a = np.asarray(a, np.float32).reshape(DEPTH, 64, 64, 16).transpose(0, 2, 1, 3)
        return np.ascontiguousarray(np.concatenate([a, a], axis=1))

    def cdup(a):
        a = np.asarray(a, np.float32).reshape(DEPTH, 64, 16, 64).transpose(0, 3, 1, 2)
        return np.ascontiguousarray(np.concatenate([a, a], axis=1))

    s["s5_bre"] = bdup(inp["s5_b_re"])
    s["s5_bim"] = bdup(inp["s5_b_im"])
    s["s5_cre"] = cdup(inp["s5_c_re"])
    s["s5_cim"] = cdup(inp["s5_c_im"])
    dv = np.asarray(inp["s5_d"], np.float32).reshape(DEPTH, 1, 512)
    s["s5_dv"] = np.ascontiguousarray(np.broadcast_to(dv, (DEPTH, 128, 512)))
    s["w_glu"] = f(inp["s5_w_glu"])
    s["w_out"] = f(inp["w_out"])
    s["mlp_w1"] = f(inp["mlp_w1"])
    s["mlp_w2"] = f(inp["mlp_w2"])
    s.update(const_inputs())
    return s


def core_inputs(inp, b):
    m = {}
    m["xT"] = np.ascontiguousarray(np.asarray(inp["x"][b], np.float32).T)
    m["ctxT"] = np.ascontiguousarray(np.asarray(inp["ctx"][b], np.float32).T)
    cc = np.stack([np.asarray(inp["c"][b], np.float32), np.asarray(inp["c_ctx"], np.float32)], axis=-1)
    m["cc"] = np.ascontiguousarray(cc.reshape(KC, 128, 2).transpose(1, 0, 2))
    return m


_CACHE = {}


def kernel(**inputs):
    if "nc" not in _CACHE:
        _CACHE["nc"] = Builder().build()
    nc = _CACHE["nc"]
    sh = shared_inputs(inputs)
    nb = inputs["x"].shape[0]
    in_maps = []
    for b in range(nb):
        m = dict(sh)
        m.update(core_inputs(inputs, b))
        in_maps.append(m)
    res = run_bass_kernel_spmd(nc, in_maps, core_ids=list(range(nb)))
    out = np.stack([np.ascontiguousarray(r["outT"].T) for r in res.results], axis=0)
    return out.astype(np.float32)


def phase3(self, layer):
    A, S = self.A, self.S
    last = layer == DEPTH - 1
    m0 = A.mark()
    N = 512
    NKT = NTOK // 128
    WUQ = A.alloc([2, 768], BF16)
    WUQR = A.alloc([2, 8, 96], BF16)
    WUKV = A.alloc([2, 1024], BF16)
    oW = Obj()
    oWR = Obj()
    KT = A.alloc([8, NTOK], BF16)
    oKT = [Obj() for _ in range(8)]
    VA = A.alloc([NKT, 8, 65], BF16)
    oVA = Obj()
    ZB = [A.alloc([2, N], BF16) for _ in range(2)]
    oZB = [Obj(), Obj()]
    COS = [A.alloc([N], F32) for _ in range(2)]
    SIN = [A.alloc([N], F32) for _ in range(2)]
    oROPE = [Obj(), Obj()]
    QT = [A.alloc([N], BF16) for _ in range(2)]
    oQT = [Obj(), Obj()]
    SQ = A.alloc([N], BF16)
    oSQ = Obj()
    T1 = A.alloc([N], F32)
    T2 = A.alloc([N], F32)
    oT1, oT2 = Obj(), Obj()
    PT = [A.alloc([2, N], BF16) for _ in range(3)]
    oPT = [Obj() for _ in range(3)]
    OTM = [A.alloc([4, 512], BF16) for _ in range(2)]
    oOTM = [Obj(), Obj()]
    OFM = [A.alloc([4, N], BF16) for _ in range(2)]
    oOFM = [Obj(), Obj()]
    KMXP = A.alloc([8, 16], F32)
    KMAX = A.alloc([8], F32)
    oKMX = Obj()
    QMX = A.alloc([1], F32)
    QT1 = A.alloc([1], F32)
    QT2 = A.alloc([1], F32)
    BIAS = [A.alloc([1], F32) for _ in range(2)]
    oQMX, oQT1, oQT2 = Obj(), Obj(), Obj()
    oBIAS = [Obj(), Obj()]
    RCP = [A.alloc([4], F32) for _ in range(2)]
    oRCP = [Obj(), Obj()]

    self.load_w(WUQ, self.w_uq[layer], 2, oW)
    self.load_w(WUKV, self.w_ukv[layer], 2, oW)
    self.memset("dve", WUQR[:, :, :, 0:64], 0.0, [oWR])
    wq = WUQ.rearrange("p c (h d) -> p c h d", h=8)
    for c in range(2):
        for a in range(2):
            lo = 64 + a * 16
            self.ts("dve", WUQR[:, c, :, lo:lo + 8], wq[:, c, :, lo + 8:lo + 16], -1.0, None, ALU.mult, None, [oW], [oWR])
            self.cp("dve", WUQR[:, c, :, lo + 8:lo + 16], wq[:, c, :, lo:lo + 8], [oW], [oWR])
    self.memset("dve", VA[:, :, :, 64:65], 1.0, [oVA])
    for h in range(8):
        self.dma("sp", KT[64:96, h, :], self.KPE[:, :], (), [oKT[h]])

    pa = [0]

    def bankA():
        b = pa[0]
        pa[0] = (b + 1) % 2
        return b

    wv = WUKV.rearrange("p c (h d) -> p c h d", h=8)
    blocks = self.tiles(N)
    for bi, (is_ctx, dcol, scol, n) in enumerate(blocks):
        zb, ozb = ZB[bi % 2], oZB[bi % 2]
        self.dma("sp", zb[:, :, :n], self.ZKVN[:, dcol:dcol + n].rearrange("(j p) n -> p j n", p=128), (), [ozb])
        for h in range(8):
            b = 3 + (h % 3)
            for c in range(2):
                self.mm(self.PS[b][0:64, :n], WUKV[:, c, h * 128:h * 128 + 64], zb[:, c, :n], c == 0, c == 1,
                        [oW, ozb], [self.PSO[b]])
            self.cp("dve", KT[0:64, h, dcol:dcol + n], self.PS[b][0:64, :n], [self.PSO[b]], [oKT[h]])
        for sub in range(n // 128):
            kt = (dcol + sub * 128) // 128
            b = 6 + (sub % 2)
            psv = self.PS[b].rearrange("p (h d) -> p h d", h=8)
            for c in range(2):
                self.mm(psv, zb[:, c, sub * 128:(sub + 1) * 128], wv[:, c, :, 64:128], c == 0, c == 1, [oW, ozb], [self.PSO[b]])
            self.cp("dve", VA[:, kt, :, 0:64], psv, [self.PSO[b]], [oVA])
    for h in range(8):
        for bi, (is_ctx, dcol, scol, n) in enumerate(blocks):
            self.tt("pool", SQ[0:96, :n], KT[0:96, h, dcol:dcol + n], KT[0:96, h, dcol:dcol + n], ALU.mult, [oKT[h]], [oSQ])
            b = bankA()
            self.mm(self.PS[b][:, :n], self.ONESB[0:96, :], SQ[0:96, :n], True, True, [oSQ, self.CONST], [self.PSO[b]])
            self.S.op("dve", lambda hh, o=KMXP[:, h, bi:bi + 1], i=self.PS[b][:, :n]: hh.reduce_max(o, i, AX.X),
                      [self.PSO[b]], [oKMX])
    self.S.op("dve", lambda hh: hh.reduce_max(KMAX, KMXP[:, :, 0:len(blocks)], AX.X), [oKMX], [oKMX])

    qblocks = [t for t in blocks if not (t[0] and last)]
    sc_bank = [0]
    pt_i = [0]
    items = [(qi, h) for qi in range(len(qblocks)) for h in range(8)]

    def load_block(qi):
        is_ctx, dcol, scol, n = qblocks[qi]
        zb, ozb = ZB[qi % 2], oZB[qi % 2]
        self.dma("sp", zb[:, :, :n], self.ZQN[:, dcol:dcol + n].rearrange("(j p) n -> p j n", p=128), (), [ozb])
        if not is_ctx:
            self.dma("sp", COS[qi % 2][64:96, :n], self.c_ropec[:, scol:scol + n], (), [oROPE[qi % 2]])
            self.dma("sp", SIN[qi % 2][64:96, :n], self.c_ropes[:, scol:scol + n], (), [oROPE[qi % 2]])

    def prep_q(k):
        qi, h = items[k]
        is_ctx, dcol, scol, n = qblocks[qi]
        zb, ozb = ZB[qi % 2], oZB[qi % 2]
        qt, oqt = QT[k % 2], oQT[k % 2]
        ba = bankA()
        for c in range(2):
            self.mm(self.PS[ba][0:96, :n], WUQ[:, c, h * 96:(h + 1) * 96], zb[:, c, :n], c == 0, c == 1, [oW, ozb], [self.PSO[ba]])
        self.cp("dve", qt[0:64, :n], self.PS[ba][0:64, :n], [self.PSO[ba]], [oqt])
        if is_ctx:
            self.cp("dve", qt[64:96, :n], self.PS[ba][64:96, :n], [self.PSO[ba]], [oqt])
        else:
            bb = bankA()
            for c in range(2):
                self.mm(self.PS[bb][0:96, :n], WUQR[:, c, h, :], zb[:, c, :n], c == 0, c == 1, [oWR, ozb], [self.PSO[bb]])
            self.tt("dve", T1[64:96, :n], self.PS[ba][64:96, :n], COS[qi % 2][64:96, :n], ALU.mult,
                    [self.PSO[ba], oROPE[qi % 2]], [oT1])
            self.tt("dve", T2[64:96, :n], self.PS[bb][64:96, :n], SIN[qi % 2][64:96, :n], ALU.mult,
                    [self.PSO[bb], oROPE[qi % 2]], [oT2])
            self.tt("dve", qt[64:96, :n], T1[64:96, :n], T2[64:96, :n], ALU.add, [oT1, oT2], [oqt])
        self.tt("pool", SQ[0:96, :n], qt[0:96, :n], qt[0:96, :n], ALU.mult, [oqt], [oSQ])

    def prep_bias(k):
        qi, h = items[k]
        is_ctx, dcol, scol, n = qblocks[qi]
        bs = bankA()
        self.mm(self.PS[bs][:, :n], self.ONESB[0:96, :], SQ[0:96, :n], True, True, [oSQ, self.CONST], [self.PSO[bs]])
        self.S.op("dve", lambda hh, i=self.PS[bs][:, :n]: hh.reduce_max(QMX, i, AX.X), [self.PSO[bs]], [oQMX])
        bias, obias = BIAS[k % 2], oBIAS[k % 2]
        self.tt("dve", QT1, QMX, KMAX[:, h:h + 1], ALU.add, [oQMX, oKMX], [oQT1])
        self.ts("dve", bias, QT1, -0.5 * ATTN_SCALE, None, ALU.mult, None, [oQT1], [obias])

    load_block(0)
    prep_q(0)
    prep_bias(0)
    for k, (qi, h) in enumerate(items):
        is_ctx, dcol, scol, n = qblocks[qi]
        nq = n // 128
        kts = [0, 1] if is_ctx else list(range(NKT))
        otm, ootm = OTM[qi % 2], oOTM[qi % 2]
        qt, oqt = QT[k % 2], oQT[k % 2]
        bias, obias = BIAS[k % 2], oBIAS[k % 2]
        have_next = k + 1 < len(items)
        if have_next:
            if items[k + 1][0] != qi:
                load_block(qi + 1)
            prep_q(k + 1)
        bo = 6 + (k % 2)
        npair = len(kts) // 2
        pend = []
        bias_at = min(3, npair)
        for i in range(npair + 1):
            if i < npair:
                pj = 1 + (sc_bank[0] % 2)
                sc_bank[0] += 1
                for hf in range(2):
                    kt = kts[2 * i + hf]
                    bk = 2 * pj + hf
                    self.mm(self.PS[bk][:, :n], KT[0:96, h, kt * 128:(kt + 1) * 128], qt[0:96, :n], True, True,
                            [oKT[h], oqt], [self.PSO[bk]])
                p = pt_i[0] % 3
                pt_i[0] += 1
                src = self.PS2[pj].rearrange("p (b n) -> p b n", b=2)[:, :, :n]
                self.act(PT[p][:, :, :n], src, AF.Exp, [self.PSO[2 * pj], self.PSO[2 * pj + 1], obias], [oPT[p]],
                         bias=bias, scale=ATTN_SCALE)
                pend.append(p)
            if i >= 1:
                p = pend[i - 1]
                for hf in range(2):
                    kt = kts[2 * (i - 1) + hf]
                    for qs in range(nq):
                        first = (i == 1 and hf == 0 and qs == 0)
                        lastm = (i == npair and hf == 1)
                        self.mm(self.PS[bo][:, qs * 65:(qs + 1) * 65], PT[p][:, hf, qs * 128:(qs + 1) * 128], VA[:, kt, h, :],
                                first, lastm, [oPT[p], oVA], [self.PSO[bo]], skip_group_check=True)
            if i == bias_at and have_next:
                prep_bias(k + 1)
        pov = self.PS[bo][:, 0:nq * 65].rearrange("p (q d) -> p q d", d=65)
        rcp, orcp = RCP[k % 2], oRCP[k % 2]
        self.S.op("dve", lambda hh, o=rcp[:, 0:nq], i=pov[:, :, 64]: hh.reciprocal(o, i), [self.PSO[bo]], [orcp])
        self.tt("dve", otm[:, 0:nq, h * 64:(h + 1) * 64], pov[:, :, 0:64], bc(rcp[:, 0:nq].unsqueeze(2), [128, nq, 64]),
                ALU.mult, [self.PSO[bo], orcp], [ootm])
        if h == 7:
            ofm, oofm = OFM[qi % 2], oOFM[qi % 2]
            for ft in range(4):
                bt = bankA()
                psb = self.PS[bt].bitcast(BF16)
                for qs in range(nq):
                    self.S.op("pe", lambda hh, o=psb[:, qs * 128:(qs + 1) * 128], i=otm[:, qs, ft * 128:(ft + 1) * 128]:
                              hh.transpose(o, i, self.IDENTB), [ootm, self.CONST], [self.PSO[bt]])
                self.cp("dve", ofm[:, ft, :n], psb[:, 0:n], [self.PSO[bt]], [oofm])
            self.dma("sp", self.ATTO[:, dcol:dcol + n].rearrange("(j p) n -> p j n", p=128), ofm[:, :, :n], [oofm], [])
    S.barrier()
    A.reset(m0)


Builder.phase3 = phase3


def phase2(self, layer):
    A, S = self.A, self.S
    m0 = A.mark()
    PI_LO = 3.14159
    al = lambda sh: A.alloc(sh, F32)
    self.IDENT, self.JPERM, self.MASKF, self.MASKB = al([128]), al([128]), al([128]), al([128])
    self.SELC = A.alloc([2, 8, 128], BF16)
    self.SELC2 = A.alloc([2, 8, 128], BF16)
    oC2 = Obj()
    c2 = [oC2]
    self.dma("sp", self.IDENT, self.c_ident, (), c2)
    self.dma("sp", self.JPERM, self.c_jperm, (), c2)
    self.dma("sp", self.MASKF, self.c_maskf, (), c2)
    self.dma("sp", self.MASKB, self.c_maskb, (), c2)
    self.dma("pool", self.SELC, self.c_selc.rearrange("p (a b c) -> p a b c", a=2, b=8), (), c2)
    self.dma("pool", self.SELC2, self.c_selc2.rearrange("p (a b c) -> p a b c", a=2, b=8), (), c2)
    SELF = A.alloc([8, 8, 128], BF16)
    for a_ in range(8):
        self.dma("pool", SELF[:, a_], self.c_self[:, a_ * 1024:(a_ + 1) * 1024].rearrange("p (b c) -> p b c", b=8), (), c2)
    UPR, UPN, URR, URN = al([9, 64]), al([9, 64]), al([9, 64]), al([9, 64])
    DNR, DNN = al([8, 64]), al([8, 64])
    HSR, HSI, HIM = al([10, 64]), al([10, 64]), al([10, 64])
    BA, BBp, CA, CB = al([64, 16]), al([64, 16]), al([64, 16]), al([64, 16])
    DV = al([512])
    oP = Obj()
    m1 = A.mark()
    ARE, AIM, LDT, DT, X1, MAG, ANG, ANGC, K1, K2, RS, SINA, COSA, ABR, ABI, ABN = (al([64]) for _ in range(16))
    T1, T2, T3, T4, DEN, RDEN, NR, FRE, FIM, RM2, D1R, D1N = (al([64]) for _ in range(12))
    BRE, BIM, CRE, CIM, BBR, BBI, TB1, TB2 = (al([64, 16]) for _ in range(8))
    P_ = [oP]

    def tt(o, a, b, op):
        self.tt("dve", o, a, b, op, P_, P_)

    def ts(o, a, s1, s2, op0, op1=None):
        self.ts("dve", o, a, s1, s2, op0, op1, P_, P_)

    def cmul(oR, oN, xR, xN, yR, yN):
        tt(T1, xR, yR, ALU.mult)
        tt(T2, xN, yN, ALU.mult)
        tt(T3, xR, yN, ALU.mult)
        tt(T4, xN, yR, ALU.mult)
        tt(oR, T1, T2, ALU.subtract)
        tt(oN, T3, T4, ALU.add)

    for dst, src in ((ARE, self.s5_are), (AIM, self.s5_aim), (LDT, self.s5_ldt), (BRE, self.s5_bre), (BIM, self.s5_bim),
                     (CRE, self.s5_cre), (CIM, self.s5_cim), (DV, self.s5_dv)):
        self.dma("sp", dst, src[layer], (), P_)
    self.act(DT, LDT, AF.Exp, P_, P_)
    tt(X1, DT, ARE, ALU.mult)
    self.act(MAG, X1, AF.Exp, P_, P_)
    tt(ANG, DT, AIM, ALU.mult)
    ts(ANGC, ANG, math.pi / 2.0, None, ALU.add)

    def sin_of(dst, ang):
        ts(K1, ang, 1.0 / TWO_PI, MAGIC, ALU.mult, ALU.add)
        ts(K2, K1, -MAGIC, None, ALU.add)
        self.stt(RS, K2, -TWO_PI, ang, ALU.mult, ALU.add, P_, P_)
        ts(RS, RS, -PI_LO, PI_LO, ALU.max, ALU.min)
        self.act(dst, RS, AF.Sin, P_, P_)

    sin_of(SINA, ANG)
    sin_of(COSA, ANGC)
    tt(ABR, MAG, COSA, ALU.mult)
    tt(ABI, MAG, SINA, ALU.mult)
    ts(ABN, ABI, -1.0, None, ALU.mult)
    tt(T1, ARE, ARE, ALU.mult)
    tt(T2, AIM, AIM, ALU.mult)
    tt(DEN, T1, T2, ALU.add)
    self.recip(RDEN, DEN, P_, P_)
    ts(NR, ABR, -1.0, None, ALU.add)
    tt(T1, NR, ARE, ALU.mult)
    tt(T2, ABI, AIM, ALU.mult)
    tt(T1, T1, T2, ALU.add)
    tt(FRE, T1, RDEN, ALU.mult)
    tt(T1, ABI, ARE, ALU.mult)
    tt(T2, NR, AIM, ALU.mult)
    tt(T1, T1, T2, ALU.subtract)
    tt(FIM, T1, RDEN, ALU.mult)
    fre_b = bc(FRE.unsqueeze(2), [128, 64, 16])
    fim_b = bc(FIM.unsqueeze(2), [128, 64, 16])
    tt(TB1, BRE, fre_b, ALU.mult)
    tt(TB2, BIM, fim_b, ALU.mult)
    tt(BBR, TB1, TB2, ALU.subtract)
    tt(TB1, BIM, fre_b, ALU.mult)
    tt(TB2, BRE, fim_b, ALU.mult)
    tt(BBI, TB1, TB2, ALU.add)
    lo, hi = slice(0, 64), slice(64, 128)
    self.cp("dve", BA[lo], BBR[lo], P_, P_)
    self.cp("dve", BA[hi], BBI[hi], P_, P_)
    self.cp("dve", BBp[lo], BBI[lo], P_, P_)
    ts(BBp[hi], BBR[hi], -1.0, None, ALU.mult)
    self.cp("dve", CA[lo], CRE[lo], P_, P_)
    ts(CA[hi], CIM[hi], -1.0, None, ALU.mult)
    self.cp("dve", CB[lo], CIM[lo], P_, P_)
    self.cp("dve", CB[hi], CRE[hi], P_, P_)
    self.memset("dve", UPR[:, 0, :], 1.0, P_)
    self.memset("dve", UPN[:, 0, :], 0.0, P_)
    self.cp("dve", UPR[:, 1, :], ABR, P_, P_)
    self.cp("dve", UPN[:, 1, :], ABN, P_, P_)
    for e in range(2, 9):
        cmul(UPR[:, e, :], UPN[:, e, :], UPR[:, e - 1, :], UPN[:, e - 1, :], ABR, ABN)
    tt(T1, MAG, MAG, ALU.mult)
    self.recip(RM2, T1, P_, P_)
    tt(D1R, ABR, RM2, ALU.mult)
    tt(D1N, ABI, RM2, ALU.mult)
    self.memset("dve", DNR[:, 0, :], 1.0, P_)
    self.memset("dve", DNN[:, 0, :], 0.0, P_)
    self.cp("dve", DNR[:, 1, :], D1R, P_, P_)
    self.cp("dve", DNN[:, 1, :], D1N, P_, P_)
    for e in range(2, 8):
        cmul(DNR[:, e, :], DNN[:, e, :], DNR[:, e - 1, :], DNN[:, e - 1, :], D1R, D1N)
    for i in range(9):
        self.cp("dve", URR[:, i, :], UPR[:, 8 - i, :], P_, P_)
        self.cp("dve", URN[:, i, :], UPN[:, 8 - i, :], P_, P_)
    self.cp("dve", HSR[:, 0, :], UPR[:, 8, :], P_, P_)
    ts(HIM[:, 0, :], UPN[:, 8, :], -1.0, None, ALU.mult)
    for j in range(9):
        tt(T1, HSR[:, j, :], HSR[:, j, :], ALU.mult)
        tt(T2, HIM[:, j, :], HIM[:, j, :], ALU.mult)
        tt(HSR[:, j + 1, :], T1, T2, ALU.subtract)
        tt(T3, HSR[:, j, :], HIM[:, j, :], ALU.mult)
        ts(HIM[:, j + 1, :], T3, 2.0, None, ALU.mult)
    self.cp("dve", HSI[lo], HIM[lo], P_, P_)
    ts(HSI[hi], HIM[hi], -1.0, None, ALU.mult)
    S.barrier()
    A.reset(m1)

    oTAB = Obj()
    ZS = [A.alloc([NTOK], BF16) for _ in range(2)]
    oZS = [Obj(), Obj()]
    FTb, XMb, YMb = al([2, 8, 128]), al([2, 8, 128]), al([2, 8, 128])
    Gb = A.alloc([2, 8, 128], BF16)
    TMP4 = al([8, 128])
    TMP5 = al([8, 128])
    Fb = A.alloc([2, 8, 128], BF16)
    MTb = A.alloc([8, 128], BF16)
    TM1, TM2, DD = al([4, 128]), al([4, 128]), al([4, 128])
    oFT, oXM, oYM, oG, oTMP4, oF, oMT, oTM = (Obj() for _ in range(8))
    FR = self.FR
    fo = [0]

    def fal(n):
        v = FR[:, fo[0]:fo[0] + n]
        fo[0] += n
        return v

    RM = [fal(1280).rearrange("p (j n) -> p j n", j=10) for _ in range(4)]
    oRM = [Obj() for _ in range(4)]
    IDENTR = fal(128)
    self.cp("dve", IDENTR, self.IDENT, [oC2], [oC2])
    RT1 = [al([10, 128]) for _ in range(2)]
    RT2 = [al([10, 128]) for _ in range(2)]
    oRT = [Obj(), Obj()]
    cpi = [0]

    def scan_lockstep(chains, ncols):
        W = ncols + 1
        for j, sh in hs_levels(ncols):
            cnt = ncols - sh
            geo = []
            for (tile, otile, rm, orm, d, psb, pcol) in chains:
                if d == 0:
                    n = cnt if cnt % 2 == 0 else cnt + 1
                    o0, r0 = sh, 0
                else:
                    c0 = 1 if cnt % 2 == 0 else 0
                    n = W - sh - c0
                    o0, r0 = c0, c0 + sh
                geo.append((n, o0, r0))
                self.mm(PS[psb][:, pcol:pcol + n], rm[:, j, :], tile[:, r0:r0 + n], True, False, [orm, otile], [PSO[psb]])
                self.mm(PS[psb][:, pcol:pcol + n], IDENTR, tile[:, o0:o0 + n], False, True, [oC2, otile], [PSO[psb]])
            for (tile, otile, rm, orm, d, psb, pcol), (n, o0, r0) in zip(chains, geo):
                eng = "act" if cpi[0] % 2 == 0 else "dve"
                cpi[0] += 1
                self.cp(eng, tile[:, o0:o0 + n], PS[psb][:, pcol:pcol + n], [PSO[psb]], [otile])
    UG = [A.alloc([544], BF16) for _ in range(2)]
    oUG = [Obj(), Obj()]
    S0L = [fal(514) for _ in range(2)]
    S1L = [fal(514) for _ in range(2)]
    S0C = [fal(36) for _ in range(2)]
    S1C = [fal(36) for _ in range(2)]
    oS0L, oS1L, oS0C, oS1C = ([Obj(), Obj()] for _ in range(4))
    SB0L = [A.alloc([514], BF16) for _ in range(2)]
    SB1L = [A.alloc([514], BF16) for _ in range(2)]
    SB0C = [A.alloc([36], BF16) for _ in range(2)]
    SB1C = [A.alloc([36], BF16) for _ in range(2)]
    oSB = [Obj(), Obj()]
    YG = A.alloc([8, 544], BF16)
    oYG = Obj()
    GO = A.alloc([NTOK], BF16)
    oGO = Obj()
    PS, PSO = self.PS, self.PSO
    for i in range(2):
        for t_, o_ in ((S0L, oS0L), (S1L, oS1L), (S0C, oS0C), (S1C, oS1C)):
            self.ts("dve", t_[i], bc(self.IDENT[:, 0:1], [128, t_[i].shape[1]]), 0.0, None, ALU.mult, None, [oC2], [o_[i]])

    def hs_levels(total):
        sh = 1
        j = 0
        res = []
        while sh < total:
            res.append((j, sh))
            sh *= 2
            j += 1
        return res

    def scan(tile, otile, rm, orm, d, ncols, psb, pcol):
        W = ncols + 1
        for j, sh in hs_levels(ncols):
            cnt = ncols - sh
            if d == 0:
                n = cnt if cnt % 2 == 0 else cnt + 1
                o0, r0 = sh, 0
            else:
                c0 = 1 if cnt % 2 == 0 else 0
                n = W - sh - c0
                o0, r0 = c0, c0 + sh
            self.mm(PS[psb][:, pcol:pcol + n], rm[:, j, :], tile[:, r0:r0 + n], True, True, [orm, otile], [PSO[psb]])
            self.tt("dve", tile[:, o0:o0 + n], tile[:, o0:o0 + n].bitcast(F32), PS[psb][:, pcol:pcol + n], ALU.add, [otile, PSO[psb]], [otile])

    import os
    P2NB = int(os.environ.get("P2NB", "4"))
    P2NG = int(os.environ.get("P2NG", "8"))
    P2RENG = os.environ.get("P2RENG", "pool")
    P2PARTS = os.environ.get("P2PARTS", "prep,sel,e,scan,y,out").split(",")
    for b in range(P2NB):
        g0 = 8 * b
        zs, ozs = ZS[b % 2], oZS[b % 2]
        self.dma("sp", zs, self.ZS5[b * 128:(b + 1) * 128, :], (), [ozs])
        specs = [
            (FTb, oFT, ((URR, URN, 1), (UPR, UPN, 0)), (BA, BBp)),
            (XMb, oXM, ((DNR, DNN, 0), (UPR, UPN, 0)), (BA, BBp)),
            (YMb, oYM, ((UPR, UPN, 0), (DNR, DNN, 0)), (CA, CB)),
            (Gb, oG, ((UPR, UPN, 1), (URR, URN, 0)), (CA, CB)),
        ]
        for OUT, oOUT, tabs, (QA, QB) in specs:
            for d in range(2):
                TR, TN, e0 = tabs[d]
                gs = slice(d * 32 + g0, d * 32 + g0 + 8)
                pr = bc(TR[:, e0:e0 + 8, gs].rearrange("p s g -> p g s").unsqueeze(3), [128, 8, 8, 16])
                pn = bc(TN[:, e0:e0 + 8, gs].rearrange("p s g -> p g s").unsqueeze(3), [128, 8, 8, 16])
                qa = bc(QA[:, gs, :].unsqueeze(2), [128, 8, 8, 16])
                qb = bc(QB[:, gs, :].unsqueeze(2), [128, 8, 8, 16])
                ov = OUT[:, d].rearrange("p g (s c) -> p g s c", c=16)
                tv = TMP4.rearrange("p g (s c) -> p g s c", c=16)
                tw = TMP5.rearrange("p g (s c) -> p g s c", c=16)
                self.tt("dve", tv, pn, qb, ALU.mult, [oTAB], [oTMP4])
                self.tt("dve", tw, pr, qa, ALU.mult, [oTAB], [oTMP4])
                self.tt("dve", ov, tw, tv, ALU.add, [oTMP4], [oOUT])
        k = 0
        for d in range(2):
            for g4 in range(2):
                bk = 5 + (k % 2)
                k += 1
                for i in range(4):
                    gl = g4 * 4 + i
                    self.mm(PS[bk][:, i * 128:(i + 1) * 128], FTb[:, d, gl, :], self.IDENT, True, True, [oFT, oC2], [PSO[bk]])
                self.cp("act", Fb[:, d, g4 * 4:(g4 + 1) * 4, :].rearrange("p g n -> p (g n)"), PS[bk][:, 0:512], [PSO[bk]], [oF])
        for g4 in range(2):
            for i in range(4):
                gl = g4 * 4 + i
                self.mm(PS[7][:, i * 128:(i + 1) * 128], XMb[:, 0, gl, :], YMb[:, 0, gl, :], True, True, [oXM, oYM], [PSO[7]])
            for i in range(4):
                gl = g4 * 4 + i
                self.mm(PS[6][:, i * 128:(i + 1) * 128], XMb[:, 1, gl, :], YMb[:, 1, gl, :], True, True, [oXM, oYM], [PSO[6]])
            pa = PS[7][:, 0:512].rearrange("p (g n) -> p g n", g=4)
            pb = PS[6][:, 0:512].rearrange("p (g n) -> p g n", g=4)
            self.tt("dve", TM1, pa, bc(self.MASKF.unsqueeze(1), [128, 4, 128]), ALU.mult, [PSO[7], oC2], [oTM])
            self.tt("dve", TM2, pb, bc(self.MASKB.unsqueeze(1), [128, 4, 128]), ALU.mult, [PSO[6], oC2], [oTM])
            self.tt("dve", TM1, TM1, TM2, ALU.add, [oTM], [oTM])
            gb = g0 + g4 * 4
            idv = bc(self.IDENT.rearrange("p (t c) -> p t c", c=16).unsqueeze(1), [128, 4, 8, 16])
            dvv = bc(DV[:, gb * 16:(gb + 4) * 16].rearrange("p (g c) -> p g c", c=16).unsqueeze(2), [128, 4, 8, 16])
            self.tt("dve", DD.rearrange("p g (t c) -> p g t c", c=16), idv, dvv, ALU.mult, [oC2, oTAB], [oTM])
            self.tt("dve", MTb[:, g4 * 4:(g4 + 1) * 4, :], TM1, DD, ALU.add, [oTM], [oMT])
        for gp in range(0, P2NG, 2):
            pair = [gp, gp + 1]
            ctxc, latc = [], []
            info = {}
            for gl in pair:
                q, par = gl // 2, gl % 2
                rows = slice(32 * q, 32 * q + 32) if q < 3 else slice(64, 128)
                SEL = self.SELC if q < 3 else self.SELC2
                u, ou = UG[gl % 2], oUG[gl % 2]
                rms = []
                for d in range(2):
                    ri = 2 * (gl % 2) + d
                    dg = d * 32 + g0 + gl
                    rt1, rt2, ort = RT1[d], RT2[d], oRT[d]
                    reng = "dve" if d == 0 else "pool"
                    self.tt(reng, rt1, bc(self.IDENT.unsqueeze(1), [128, 10, 128]), bc(HSR[:, :, dg:dg + 1], [128, 10, 128]),
                            ALU.mult, [oC2, oTAB], [ort])
                    self.tt(reng, rt2, bc(self.JPERM.unsqueeze(1), [128, 10, 128]), bc(HSI[:, :, dg:dg + 1], [128, 10, 128]),
                            ALU.mult, [oC2, oTAB], [ort])
                    self.tt(reng, RM[ri], rt1, rt2, ALU.add, [ort], [oRM[ri]])
                    rms.append((RM[ri], oRM[ri]))
                zl = zs[:, CTX:NTOK].rearrange("p (T s k) -> p s T k", T=8, s=8)
                pu = PS[0][:, 0:512].rearrange("p (T k) -> p T k", T=8)
                for s in range(8):
                    self.mm(pu, SELF[:, gl, s, :], zl[:, s], s == 0, s == 7, [oC2, ozs], [PSO[0]])
                for s in range(8):
                    self.mm(PS[1][:, 0:32], SELF[:, gl, s, :], zs[:, s * 32:(s + 1) * 32], s == 0, s == 7, [oC2, ozs], [PSO[1]])
                self.cp("act", u[:, 32:544], PS[0][:, 0:512], [PSO[0]], [ou])
                self.cp("act", u[:, 0:32], PS[1][:, 0:32], [PSO[1]], [ou])
                s0l, s1l, s0c, s1c = S0L[gl % 2], S1L[gl % 2], S0C[gl % 2], S1C[gl % 2]
                os0l, os1l, os0c, os1c = oS0L[gl % 2], oS1L[gl % 2], oS0C[gl % 2], oS1C[gl % 2]
                bl0, bl1 = (2, 3) if gl % 2 == 0 else (5, 6)
                cb = 0 if gl % 2 == 0 else 256
                self.mm(PS[4][:, cb:cb + 32], Fb[:, 0, gl, :], u[:, 0:32], True, True, [oF, ou], [PSO[4]])
                self.mm(PS[4][:, cb + 64:cb + 96], Fb[:, 1, gl, :], u[:, 0:32], True, True, [oF, ou], [PSO[4]])
                self.mm(PS[bl0][:, 0:512], Fb[:, 0, gl, :], u[:, 32:544], True, True, [oF, ou], [PSO[bl0]])
                self.mm(PS[bl1][:, 0:512], Fb[:, 1, gl, :], u[:, 32:544], True, True, [oF, ou], [PSO[bl1]])
                self.cp("act", s0c[:, 1:33], PS[4][:, cb:cb + 32], [PSO[4]], [os0c])
                self.cp("act", s1c[:, 1:33], PS[4][:, cb + 64:cb + 96], [PSO[4]], [os1c])
                self.cp("act", s0l[:, 1:513], PS[bl0][:, 0:512], [PSO[bl0]], [os0l])
                self.cp("act", s1l[:, 1:513], PS[bl1][:, 0:512], [PSO[bl1]], [os1l])
                ctxc.append((s0c, os0c, rms[0][0], rms[0][1], 0, 4, cb + 128))
                ctxc.append((s1c, os1c, rms[1][0], rms[1][1], 1, 4, cb + 192))
                latc.append((s0l, os0l, rms[0][0], rms[0][1], 0, bl0, 0))
                latc.append((s1l, os1l, rms[1][0], rms[1][1], 1, bl1, 0))
                info[gl] = (u, ou, s0l, s1l, s0c, s1c, os0l, os1l, os0c, os1c)
            scan_lockstep(ctxc, 33)
            for gl in pair:
                u, ou, s0l, s1l, s0c, s1c, os0l, os1l, os0c, os1c = info[gl]
                self.cp("dve", s0l[:, 0:1], s0c[:, 32:33].bitcast(F32), [os0c], [os0l])
                self.cp("dve", s1l[:, 513:514], s1c[:, 1:2].bitcast(F32), [os1c], [os1l])
            scan_lockstep(latc, 513)
            for gl in pair:
                u, ou, s0l, s1l, s0c, s1c, os0l, os1l, os0c, os1c = info[gl]
                sb0l, sb1l, sb0c, sb1c, osb = SB0L[gl % 2], SB1L[gl % 2], SB0C[gl % 2], SB1C[gl % 2], oSB[gl % 2]
                self.cp("pool", sb0c, s0c.bitcast(F32), [os0c], [osb])
                self.cp("pool", sb1c, s1c.bitcast(F32), [os1c], [osb])
                self.cp("act", sb0l, s0l.bitcast(F32), [os0l], [osb])
                self.cp("pool", sb1l, s1l.bitcast(F32), [os1l], [osb])
                self.mm(PS[1][:, 0:32], MTb[:, gl, :], u[:, 0:32], True, False, [oMT, ou], [PSO[1]])
                self.mm(PS[1][:, 0:32], Gb[:, 0, gl, :], sb0c[:, 0:32], False, False, [oG, osb], [PSO[1]])
                self.mm(PS[1][:, 0:32], Gb[:, 1, gl, :], sb1c[:, 2:34], False, True, [oG, osb], [PSO[1]])
                self.mm(PS[7][:, 0:512], MTb[:, gl, :], u[:, 32:544], True, False, [oMT, ou], [PSO[7]])
                self.mm(PS[7][:, 0:512], Gb[:, 0, gl, :], sb0l[:, 0:512], False, False, [oG, osb], [PSO[7]])
                self.mm(PS[7][:, 0:512], Gb[:, 1, gl, :], sb1l[:, 2:514], False, True, [oG, osb], [PSO[7]])
                self.act(YG[:, gl, 0:32], PS[1][:, 0:32], AF.Gelu_apprx_tanh, [PSO[1]], [oYG])
                self.act(YG[:, gl, 32:544], PS[7][:, 0:512], AF.Gelu_apprx_tanh, [PSO[7]], [oYG])
        for t in range(8 if "out" in P2PARTS else 0):
            q, par = t // 2, t % 2
            rows = slice(32 * q, 32 * q + 32) if q < 3 else slice(64, 128)
            SEL = self.SELC if q < 3 else self.SELC2
            for gl in range(8):
                self.mm(PS[7][:, 0:512], SELF[:, t, gl, :], YG[:, gl, 32:544], gl == 0, gl == 7, [oC2, oYG], [PSO[7]])
            for gl in range(8):
                self.mm(PS[1][:, 0:32], SELF[:, t, gl, :], YG[:, gl, 0:32], gl == 0, gl == 7, [oC2, oYG], [PSO[1]])
            self.cp("dve" if t % 2 == 0 else "act", GO[:, CTX + t:NTOK:8], PS[7][:, 0:512], [PSO[7]], [oGO])
            self.cp("dve", GO[:, t:CTX:8], PS[1][:, 0:32], [PSO[1]], [oGO])
        if "out" in P2PARTS:
            self.dma("sp", self.GS5[b * 128:(b + 1) * 128, :], GO, [oGO], [])
    S.barrier()
    A.reset(m0)


Builder.phase2 = phase2
```
